# Optimizing a Trainium2 kernel written in Bass

```python
import math
import jax, jax.numpy as jnp
from jax import lax
import numpy as np

D_MODEL = 1024
BATCH = 8
SEQ = 2048
DEPTH = 1
DEC_BATCH = 128
DEC_SEQ = 1
PAST_LEN = 16384
PAGE_SIZE = 128

MIX_WIDTH = D_MODEL
SSM_WIDTH = MIX_WIDTH // 2
POOL_WIDTH = MIX_WIDTH - SSM_WIDTH
SSM_GROUP_CH = 16
SSM_GROUPS = SSM_WIDTH // SSM_GROUP_CH
SSM_STATE = 64
POOL_WINDOWS = (2, 4, 8, 16)
POOL_GROUPS = len(POOL_WINDOWS)
POOL_GROUP_CH = POOL_WIDTH // POOL_GROUPS
POOL_BUF = max(POOL_WINDOWS) - 1
IN_WIDTH = 2 * SSM_WIDTH + 2 * POOL_WIDTH
DN_ALPHA = (2.0 * DEPTH) ** 0.25
DN_BETA = (8.0 * DEPTH) ** -0.25
LN_EPS = 1e-5
DT_MIN = 1e-3
DT_MAX = 1e-1

kernel_name = "hybrid_s5_pool_deepnorm_step"


def _layer_norm(x, g, b):
    x = x.astype(jnp.float32)
    mu = jnp.mean(x, axis=-1, keepdims=True)
    var = jnp.mean(jnp.square(x - mu), axis=-1, keepdims=True)
    return (x - mu) * lax.rsqrt(var + LN_EPS) * g.astype(jnp.float32) + b.astype(jnp.float32)


def _cmul(ar, ai, br, bi):
    return ar * br - ai * bi, ar * bi + ai * br


def _scan_combine(e1, e2):
    a1r, a1i, b1r, b1i = e1
    a2r, a2i, b2r, b2i = e2
    ar, ai = _cmul(a2r, a2i, a1r, a1i)
    br, bi = _cmul(a2r, a2i, b1r, b1i)
    return ar, ai, br + b2r, bi + b2i


def _s5_mixer(u, h0r, h0i, lam_re, lam_im, log_dt, b_re, b_im, c_re, c_im, d):
    bsz, L, _ = u.shape
    f32 = jnp.float32
    lam_re = lam_re.astype(f32); lam_im = lam_im.astype(f32)
    ug = u.reshape(bsz, L, SSM_GROUPS, SSM_GROUP_CH)
    dt = jnp.exp(log_dt.astype(f32))[:, None]
    mag = jnp.exp(lam_re * dt)
    ar, ai = mag * jnp.cos(lam_im * dt), mag * jnp.sin(lam_im * dt)
    den = lam_re * lam_re + lam_im * lam_im
    nr = ar - 1.0
    qr = (nr * lam_re + ai * lam_im) / den
    qi = (ai * lam_re - nr * lam_im) / den
    b_re = b_re.astype(f32); b_im = b_im.astype(f32)
    bbar_re = qr[..., None] * b_re - qi[..., None] * b_im
    bbar_im = qr[..., None] * b_im + qi[..., None] * b_re
    bu_re = jnp.einsum('blgc,gnc->blgn', ug, bbar_re)
    bu_im = jnp.einsum('blgc,gnc->blgn', ug, bbar_im)
    a_re = jnp.broadcast_to(ar, bu_re.shape)
    a_im = jnp.broadcast_to(ai, bu_im.shape)
    _, _, hr, hi = lax.associative_scan(_scan_combine, (a_re, a_im, bu_re, bu_im), axis=1)
    k = jnp.arange(1, L + 1, dtype=f32)[:, None, None]
    pmag = jnp.exp(lam_re * dt * k)
    ph = lam_im * dt * k
    pr, pi_ = pmag * jnp.cos(ph), pmag * jnp.sin(ph)
    h0r = h0r.astype(f32)[:, None]; h0i = h0i.astype(f32)[:, None]
    hr = hr + pr * h0r - pi_ * h0i
    hi = hi + pr * h0i + pi_ * h0r
    y = (jnp.einsum('blgn,gcn->blgc', hr, c_re.astype(f32))
         - jnp.einsum('blgn,gcn->blgc', hi, c_im.astype(f32)))
    y = y.reshape(bsz, L, SSM_WIDTH) + d.astype(f32) * u
    return y, hr[:, -1], hi[:, -1]


def _pool_mixer(u, prefix, start_pos, w_pool, scale):
    bsz, L, _ = u.shape
    z = jnp.concatenate([prefix.astype(jnp.float32), u], axis=1)
    cs = jnp.concatenate([jnp.zeros((bsz, 1, POOL_WIDTH), jnp.float32),
                          jnp.cumsum(z, axis=1)], axis=1)
    pos = start_pos + jnp.arange(L)
    outs = []
    for g, w in enumerate(POOL_WINDOWS):
        lo_c, hi_c = g * POOL_GROUP_CH, (g + 1) * POOL_GROUP_CH
        s_hi = cs[:, POOL_BUF + 1:POOL_BUF + 1 + L, lo_c:hi_c]
        s_lo = cs[:, POOL_BUF + 1 - w:POOL_BUF + 1 - w + L, lo_c:hi_c]
        cnt = jnp.minimum(pos + 1, w).astype(jnp.float32)[None, :, None]
        outs.append((s_hi - s_lo) / cnt - u[..., lo_c:hi_c])
    pooled = jnp.stack(outs, axis=2)
    mixed = jnp.einsum('blgc,gcd->blgd', pooled, w_pool.astype(jnp.float32))
    mixed = mixed.reshape(bsz, L, POOL_WIDTH) * scale.astype(jnp.float32)
    return mixed, z[:, -POOL_BUF:]


def _layer(x, h0r, h0i, pool_prefix, start_pos, w_in, lam_re, lam_im, log_dt, b_re, b_im,
           c_re, c_im, d, glu_w, glu_b, pool_w, pool_scale, w_out, ln_g, ln_b):
    h = x.astype(jnp.float32)
    proj = h @ w_in.astype(jnp.float32)
    s_in, s_gate, p_in, p_gate = jnp.split(
        proj, [SSM_WIDTH, 2 * SSM_WIDTH, 2 * SSM_WIDTH + POOL_WIDTH], axis=-1)
    sy, hr, hi = _s5_mixer(s_in, h0r, h0i, lam_re, lam_im, log_dt, b_re, b_im, c_re, c_im, d)
    sy = jax.nn.gelu(sy)
    sy = sy * jax.nn.sigmoid(sy @ glu_w.astype(jnp.float32) + glu_b.astype(jnp.float32))
    py, buf = _pool_mixer(p_in, pool_prefix, start_pos, pool_w, pool_scale)
    mixed = jnp.concatenate([sy * jax.nn.silu(s_gate), py * jax.nn.silu(p_gate)], axis=-1)
    out = mixed @ w_out.astype(jnp.float32)
    y = _layer_norm(DN_ALPHA * h + out, ln_g, ln_b)
    return y, hr, hi, buf


def setup_inputs(seed: int = 0) -> dict:
    key = jax.random.key(seed)
    ks = jax.random.split(key, 24)
    f32 = jnp.float32
    n = jnp.arange(SSM_STATE, dtype=f32)
    lam_re = -0.5 + 0.01 * jax.random.normal(ks[0], (DEPTH, SSM_GROUPS, SSM_STATE), f32)
    lam_im = math.pi * n + 0.01 * jax.random.normal(ks[1], (DEPTH, SSM_GROUPS, SSM_STATE), f32)
    log_dt = jax.random.uniform(ks[2], (DEPTH, SSM_GROUPS), f32,
                                math.log(DT_MIN), math.log(DT_MAX))
    return {
        "x_prompt": jax.random.normal(ks[3], (BATCH, SEQ, D_MODEL), f32),
        "x_sample": jax.random.normal(ks[4], (DEC_BATCH, DEC_SEQ, D_MODEL), f32),
        "state_ssm_re": 0.5 * jax.random.normal(ks[5], (DEPTH, DEC_BATCH, SSM_GROUPS, SSM_STATE), f32),
        "state_ssm_im": 0.5 * jax.random.normal(ks[6], (DEPTH, DEC_BATCH, SSM_GROUPS, SSM_STATE), f32),
        "state_pool": jax.random.normal(ks[7], (DEPTH, DEC_BATCH, POOL_BUF, POOL_WIDTH), f32),
        "w_in": jax.random.normal(ks[8], (DEPTH, D_MODEL, IN_WIDTH), f32) * D_MODEL ** -0.5,
        "ssm_lambda_re": lam_re,
        "ssm_lambda_im": lam_im,
        "ssm_log_dt": log_dt,
        "ssm_b_re": jax.random.normal(ks[9], (DEPTH, SSM_GROUPS, SSM_STATE, SSM_GROUP_CH), f32) * (2 * SSM_GROUP_CH) ** -0.5,
        "ssm_b_im": jax.random.normal(ks[10], (DEPTH, SSM_GROUPS, SSM_STATE, SSM_GROUP_CH), f32) * (2 * SSM_GROUP_CH) ** -0.5,
        "ssm_c_re": jax.random.normal(ks[11], (DEPTH, SSM_GROUPS, SSM_GROUP_CH, SSM_STATE), f32) * (2 * SSM_STATE) ** -0.5,
        "ssm_c_im": jax.random.normal(ks[12], (DEPTH, SSM_GROUPS, SSM_GROUP_CH, SSM_STATE), f32) * (2 * SSM_STATE) ** -0.5,
        "ssm_d": jax.random.normal(ks[13], (DEPTH, SSM_WIDTH), f32),
        "glu_w": jax.random.normal(ks[14], (DEPTH, SSM_WIDTH, SSM_WIDTH), f32) * SSM_WIDTH ** -0.5,
        "glu_b": 0.01 * jax.random.normal(ks[15], (DEPTH, SSM_WIDTH), f32),
        "pool_w": jax.random.normal(ks[16], (DEPTH, POOL_GROUPS, POOL_GROUP_CH, POOL_GROUP_CH), f32) * POOL_GROUP_CH ** -0.5,
        "pool_scale": 1.0 + 0.02 * jax.random.normal(ks[17], (DEPTH, POOL_WIDTH), f32),
        "w_out": jax.random.normal(ks[18], (DEPTH, MIX_WIDTH, D_MODEL), f32) * (MIX_WIDTH ** -0.5 * DN_BETA),
        "ln_g": 1.0 + 0.02 * jax.random.normal(ks[19], (DEPTH, D_MODEL), f32),
        "ln_b": 0.01 * jax.random.normal(ks[20], (DEPTH, D_MODEL), f32),
    }


def reference(x_prompt, x_sample, state_ssm_re, state_ssm_im, state_pool, w_in,
              ssm_lambda_re, ssm_lambda_im, ssm_log_dt, ssm_b_re, ssm_b_im, ssm_c_re, ssm_c_im,
              ssm_d, glu_w, glu_b, pool_w, pool_scale, w_out, ln_g, ln_b):
    out_dtype = x_prompt.dtype
    hp = x_prompt.astype(jnp.float32)
    hs = x_sample.astype(jnp.float32)
    zeros_h = jnp.zeros((x_prompt.shape[0], SSM_GROUPS, SSM_STATE), jnp.float32)
    zeros_buf = jnp.zeros((x_prompt.shape[0], POOL_BUF, POOL_WIDTH), jnp.float32)
    p_re, p_im, p_buf, s_re, s_im, s_buf = [], [], [], [], [], []
    for l in range(DEPTH):
        weights = (w_in[l], ssm_lambda_re[l], ssm_lambda_im[l], ssm_log_dt[l], ssm_b_re[l],
                   ssm_b_im[l], ssm_c_re[l], ssm_c_im[l], ssm_d[l], glu_w[l], glu_b[l],
                   pool_w[l], pool_scale[l], w_out[l], ln_g[l], ln_b[l])
        hp, hr, hi, buf = _layer(hp, zeros_h, zeros_h, zeros_buf, 0, *weights)
        p_re.append(hr); p_im.append(hi); p_buf.append(buf)
        hs, hr, hi, buf = _layer(hs, state_ssm_re[l], state_ssm_im[l], state_pool[l], PAST_LEN, *weights)
        s_re.append(hr); s_im.append(hi); s_buf.append(buf)
    y_prompt = hp.astype(out_dtype)
    y_sample = hs.astype(out_dtype)
    new_ssm_re_prompt = jnp.stack(p_re)
    new_ssm_im_prompt = jnp.stack(p_im)
    new_pool_prompt = jnp.stack(p_buf)
    new_ssm_re_sample = jnp.stack(s_re)
    new_ssm_im_sample = jnp.stack(s_im)
    new_pool_sample = jnp.stack(s_buf)
    return (y_prompt, y_sample, new_ssm_re_prompt, new_ssm_im_prompt, new_pool_prompt,
            new_ssm_re_sample, new_ssm_im_sample, new_pool_sample)
```

```python
import math
import numpy as np
import concourse.bass as bass
import concourse.mybir as mybir
from concourse.bass_utils import run_bass_kernel_spmd
from contextlib import ExitStack

F32 = mybir.dt.float32
BF16 = mybir.dt.bfloat16
I32 = mybir.dt.int32
AF = mybir.ActivationFunctionType
ALU = mybir.AluOpType
AX = mybir.AxisListType

ENGS = ["pe", "act", "dve", "pool", "sp"]

D_MODEL = 1024
SEQ = 2048
NS = 16
NTOK = SEQ + NS
G = 32
TCH = 16
NCH = SEQ // TCH
POOL_WINDOWS = (2, 4, 8, 16)
DN_ALPHA = 2.0 ** 0.25
LN_EPS = 1e-5
TWO_PI = 2.0 * math.pi
MAGIC = 12582912.0
import os
KSTOP = int(os.environ.get('KSTOP', '0'))
KSUB = int(os.environ.get('KSUB', '0'))


class Prog:
    def __init__(self, nc, stack):
        self.nc = nc
        self.stack = stack
        self.q = {e: [] for e in ENGS}
        self.esem = {e: stack.enter_context(nc.semaphore("se_" + e)) for e in ENGS}
        self.cnt = {e: 0 for e in ENGS}
        self.waited = {e: {} for e in ENGS}
        self.res_w = {}
        self.res_r = {}
        self.dsem = {}
        self.dcnt = {}
        self.rec = None

    def _semof(self, key):
        if key[0] == "eng":
            return self.esem[key[1]]
        return self.dsem[key[1]]

    def _deps(self, reads, writes):
        deps = []
        for r in reads:
            t = self.res_w.get(r)
            if t is not None:
                deps.append(t)
        for w in writes:
            t = self.res_w.get(w)
            if t is not None:
                deps.append(t)
            deps.extend(self.res_r.get(w, []))
        return deps

    def _emit_waits(self, eng, deps):
        need = {}
        for (k, v) in deps:
            if k[0] == "dma":
                v = max(v, 16 * self.dcnt[k[1]])
            if k == ("eng", "pe") and eng == "pe":
                continue
            if v > need.get(k, 0):
                need[k] = v
        for k, v in need.items():
            if self.waited[eng].get(k, 0) >= v:
                continue
            self.waited[eng][k] = v
            sem = self._semof(k)
            self.q[eng].append(lambda h, sem=sem, v=v: h.wait_ge(sem, v))

    def _update(self, tok, reads, writes):
        for w in writes:
            self.res_w[w] = tok
            self.res_r[w] = []
        for r in reads:
            if r in writes:
                continue
            self.res_r.setdefault(r, []).append(tok)

    def record(self, f, *a, **k):
        assert self.rec is None
        self.rec = []
        f(*a, **k)
        r = self.rec
        self.rec = None
        return r

    def replay(self, chains):
        seen = {}
        for ci, ch in enumerate(chains):
            for kind, a, k in ch:
                for w in k["writes"]:
                    if w.startswith("ps"):
                        assert seen.setdefault(w, ci) == ci, ("PSUM bank shared between interleaved chains", w)
        idx = [0] * len(chains)
        alive = True
        while alive:
            alive = False
            for ci, ch in enumerate(chains):
                if idx[ci] < len(ch):
                    kind, a, k = ch[idx[ci]]
                    idx[ci] += 1
                    alive = True
                    if kind == "op":
                        self.op(*a, **k)
                    else:
                        self.dma(*a, **k)

    def op(self, eng, fn, reads=(), writes=()):
        if self.rec is not None:
            self.rec.append(("op", (eng, fn), dict(reads=list(reads), writes=list(writes))))
            return None
        reads = list(reads)
        writes = list(writes)
        self._emit_waits(eng, self._deps(reads, writes))
        sem = self.esem[eng]
        self.q[eng].append(lambda h, fn=fn, sem=sem: fn(h).then_inc(sem, 1))
        self.cnt[eng] += 1
        tok = (("eng", eng), self.cnt[eng])
        self._update(tok, reads, writes)
        return tok

    def dma(self, out, in_, reads=(), writes=(), key="d", queue="sp", **kw):
        if self.rec is not None:
            k = dict(reads=list(reads), writes=list(writes), key=key, queue=queue)
            k.update(kw)
            self.rec.append(("dma", (out, in_), k))
            return None
        reads = list(reads)
        writes = list(writes)
        if key not in self.dsem:
            self.dsem[key] = self.stack.enter_context(self.nc.semaphore("sd_" + key))
            self.dcnt[key] = 0
        self._emit_waits(queue, self._deps(reads, writes))
        sem = self.dsem[key]
        self.q[queue].append(
            lambda h, out=out, in_=in_, sem=sem, kw=kw: h.dma_start(out=out, in_=in_, **kw).then_inc(sem, 16)
        )
        self.dcnt[key] += 1
        tok = (("dma", key), 16 * self.dcnt[key])
        self._update(tok, reads, writes)
        return tok

    def barrier(self):
        for e in ENGS:
            for key in self.dsem:
                v = 16 * self.dcnt[key]
                k = ("dma", key)
                if v > 0 and self.waited[e].get(k, 0) < v:
                    self.waited[e][k] = v
                    sem = self.dsem[key]
                    self.q[e].append(lambda h, sem=sem, v=v: h.wait_ge(sem, v))
            for e2 in ["pe", "act", "dve", "pool"]:
                v = self.cnt[e2]
                k = ("eng", e2)
                if e2 != e and v > 0 and self.waited[e].get(k, 0) < v:
                    self.waited[e][k] = v
                    sem = self.esem[e2]
                    self.q[e].append(lambda h, sem=sem, v=v: h.wait_ge(sem, v))
        self.res_w = {}
        self.res_r = {}

    def emit(self, block):
        qs = self.q
        self.q = {e: [] for e in ENGS}

        @block.sync
        def _(h):
            for f in qs["sp"]:
                f(h)

        @block.tensor
        def _(h):
            for f in qs["pe"]:
                f(h)

        @block.scalar
        def _(h):
            for f in qs["act"]:
                f(h)

        @block.vector
        def _(h):
            for f in qs["dve"]:
                f(h)

        @block.gpsimd
        def _(h):
            for f in qs["pool"]:
                f(h)


class Arena:
    def __init__(self, tensor, n):
        self.t = tensor
        self.n = n
        self.items = []

    def alloc(self, size, b0, b1):
        osize = size
        size = (size + 7) // 8 * 8
        off = 0
        while True:
            conf = [it for it in self.items
                    if not (it[3] < b0 or it[2] > b1) and not (it[0] + it[1] <= off or off + size <= it[0])]
            if not conf:
                break
            off = max(it[0] + it[1] for it in conf)
        assert off + size <= self.n, ("arena overflow", off, size, self.n)
        self.items.append((off, size, b0, b1))
        return self.t[:, off:off + osize]


def raw(view, free, extra_off=0):
    return bass.AP(view.tensor, view.offset + extra_off, [list(view.ap[0])] + [list(f) for f in free])


N32 = 24700
N16 = 52000


def build_program():
    nc = bass.Bass("TRN2", target_bir_lowering=False)

    def din(name, shape):
        return nc.dram_tensor(name, list(shape), F32, kind="ExternalInput").ap()

    def dout(name, shape):
        return nc.dram_tensor(name, list(shape), F32, kind="ExternalOutput").ap()

    xp = din("xp", [SEQ, D_MODEL]); xs = din("xs", [NS, D_MODEL])
    sre = din("sre", [NS, G, 64]); sim = din("sim", [NS, G, 64]); spool = din("spool", [NS, 15, 512])
    w_in = din("w_in", [1024, 2048]); lam_re = din("lam_re", [G, 64]); lam_im = din("lam_im", [G, 64])
    log_dt = din("log_dt", [1, G]); b_re = din("b_re", [G, 64, 16]); b_im = din("b_im", [G, 64, 16])
    c_re = din("c_re", [G, 16, 64]); c_im = din("c_im", [G, 16, 64]); ssm_d = din("ssm_d", [512])
    glu_w = din("glu_w", [512, 512]); glu_b = din("glu_b", [512]); pool_w = din("pool_w", [4, 128, 128])
    pool_scale = din("pool_scale", [512]); w_out = din("w_out", [1024, 1024])
    ln_g = din("ln_g", [1, 1024]); ln_b = din("ln_b", [1, 1024])

    yp = dout("yp", [SEQ, D_MODEL]); ys = dout("ys", [NS, D_MODEL])
    pre = dout("pre", [G, 64]); pim = dout("pim", [G, 64]); ppool = dout("ppool", [15, 512])
    sre_o = dout("sre_o", [NS, G, 64]); sim_o = dout("sim_o", [NS, G, 64]); spool_o = dout("spool_o", [NS, 15, 512])

    with ExitStack() as st:
        A32t = st.enter_context(nc.sbuf_tensor("A32", [128, N32], F32))
        A16t = st.enter_context(nc.sbuf_tensor("A16", [128, N16], BF16))
        PS = [st.enter_context(nc.psum_tensor("ps%d" % i, [128, 512], F32)) for i in range(8)]
        A32 = Arena(A32t, N32)
        A16 = Arena(A16t, N16)
        P = Prog(nc, st)
        bank_ctr = [0]

        def nb():
            b = bank_ctr[0] % 8
            bank_ctr[0] += 1
            return b

        def f32(shape, b0, b1):
            n = int(np.prod(shape[1:]))
            v = A32.alloc(n, b0, b1)
            return v[0:shape[0]] if shape[0] < 128 else v

        def b16(shape, b0, b1):
            n = int(np.prod(shape[1:]))
            v = A16.alloc(n, b0, b1)
            return v[0:shape[0]] if shape[0] < 128 else v

        ID = f32([128, 128], 1, 4)
        IDb = b16([128, 128], 1, 4)
        uT = b16([128, 4 * NTOK], 1, 4)
        SGscr = nc.dram_tensor("sg_scr", [128, 4, NTOK], BF16).ap()
        PMscr = nc.dram_tensor("pm_scr", [128, 4, NTOK], BF16).ap()
        uT3 = uT.rearrange("p (q t) -> p q t", q=4)
        HALFPI = f32([128, 1], 1, 4)
        EPS = f32([128, 1], 1, 4)
        SGN = f32([128, 1], 1, 4)
        NSGN = f32([128, 1], 1, 4)
        M0 = f32([128, 1], 1, 4)
        M1 = f32([128, 1], 1, 4)
        Dv = f32([128, 4], 1, 4)
        PSC = f32([128, 4], 1, 4)
        GB = f32([128, 4], 1, 4)
        IOT = st.enter_context(nc.sbuf_tensor("IOT", [128, 1], I32))
        IOT2 = st.enter_context(nc.sbuf_tensor("IOT2", [128, 1], I32))
        IOTF = st.enter_context(nc.sbuf_tensor("IOTF", [128, 128], I32))
        CIDX = f32([128, 128], 1, 4)
        L2 = f32([128, 256], 1, 2)
        Dt = f32([128, G], 1, 2)
        Bcat = f32([128, G * 32], 1, 2)
        Bcat3 = Bcat.rearrange("p (g x) -> p g x", g=G)
        S_sb = f32([128, G * NCH], 2, 3)
        S3 = S_sb.rearrange("p (g c) -> p g c", g=G)
        PR = f32([128, G * 17], 1, 3); PI = f32([128, G * 17], 1, 3); PIs = f32([128, G * 17], 1, 3)
        PR3 = PR.rearrange("p (g t) -> p g t", g=G); PI3 = PI.rearrange("p (g t) -> p g t", g=G)
        PIs3 = PIs.rearrange("p (g t) -> p g t", g=G)
        BB = f32([128, G * 16], 1, 3); BBsw = f32([128, G * 16], 1, 3)
        THr = f32([128, G], 1, 3); LR = f32([128, G], 1, 3)
        Xs_sb = f32([128, G * NS], 2, 3)
        Kb = b16([128, 4 * 16 * 128], 3, 4)
        Kb4 = Kb.rearrange("p (q t m) -> p q t m", q=4, t=16)
        Yi = b16([128, G * 256], 3, 4)
        Yi3 = Yi.rearrange("p (g x) -> p g x", g=G)
        SYs = b16([128, 4 * NS], 3, 4)
        SYs3 = SYs.rearrange("p (q t) -> p q t", q=4)

        def reduce_angle(eng, dst, src, tmpb, n_dst, n_src, n_tmp):
            P.op(eng, lambda h: h.tensor_scalar(out=tmpb, in0=src, scalar1=1.0 / TWO_PI, scalar2=MAGIC, op0=ALU.mult, op1=ALU.add),
                 reads=[n_src], writes=[n_tmp])
            P.op(eng, lambda h: h.tensor_scalar(out=tmpb, in0=tmpb, scalar1=-MAGIC, scalar2=None, op0=ALU.add),
                 reads=[n_tmp], writes=[n_tmp])
            P.op(eng, lambda h: h.scalar_tensor_tensor(out=dst, in0=tmpb, scalar=-TWO_PI, in1=src, op0=ALU.mult, op1=ALU.add),
                 reads=[n_tmp, n_src], writes=[n_dst])
            P.op(eng, lambda h: h.tensor_scalar(out=dst, in0=dst, scalar1=math.pi, scalar2=-math.pi, op0=ALU.min, op1=ALU.max),
                 reads=[n_dst], writes=[n_dst])

        def sincos(sin_dst, cos_dst, ang, tmpb, n_sin, n_cos, n_ang, n_tmp):
            P.op("act", lambda h: h.activation(out=sin_dst, in_=ang, func=AF.Sin), reads=[n_ang], writes=[n_sin])
            P.op("act", lambda h: h.activation(out=tmpb, in_=ang, func=AF.Abs),
                 reads=[n_ang], writes=[n_tmp])
            P.op("act", lambda h: h.activation(out=cos_dst, in_=tmpb, func=AF.Sin, bias=HALFPI, scale=-1.0),
                 reads=[n_tmp, "HALFPI"], writes=[n_cos])


        with ExitStack() as s1:
            blk = s1.enter_context(nc.Block())
            Win = b16([128, 8 * 2048], 1, 1)
            Win3 = Win.rearrange("p (k f) -> p k f", k=8)
            Wpool = b16([128, 4 * 128], 1, 1)
            Wpool3 = Wpool.rearrange("p (g d) -> p g d", g=4)
            STG = [f32([128, 2048], 1, 1) for _ in range(2)]
            Xt = [f32([128, 1024], 1, 1) for _ in range(2)]
            xTd = [b16([128, 8 * 512], 1, 1).rearrange("p (k t) -> p k t", k=8) for _ in range(2)]
            SGt = [b16([128, 4 * 512], 1, 1).rearrange("p (q t) -> p q t", q=4) for _ in range(2)]
            PMt = [b16([128, 4 * 512], 1, 1).rearrange("p (q t) -> p q t", q=4) for _ in range(2)]
            PinT = [b16([128, 512], 1, 1) for _ in range(2)]
            V = [f32([128, 15 + 512], 1, 1) for _ in range(4)]
            W1 = f32([128, 15 + 512], 1, 1); W2 = f32([128, 15 + 512], 1, 1)
            SPG = [f32([128, 512], 1, 1) for _ in range(2)]
            TMP = [f32([128, 512], 1, 1) for _ in range(2)]
            INVC = f32([128, 16], 1, 1)
            PLAST = f32([128, 4 * 16], 1, 1)
            PLAST3 = PLAST.rearrange("p (g t) -> p g t", g=4)
            PinS = f32([128, 4 * 16], 1, 1)
            PinS3 = PinS.rearrange("p (g t) -> p g t", g=4)
            SPn = f32([128, 2 * 512], 1, 1)
            SPn3 = SPn.rearrange("p (h c) -> p h c", h=2)
            PfT = f32([128, 4 * 240], 1, 1)
            PfT3 = PfT.rearrange("p (g r) -> p g r", g=4)
            WSUM = f32([128, 16], 1, 1)
            PoS = b16([128, 16], 1, 1)
            ROW = f32([128, 512], 1, 1)

            LamR = f32([128, G], 1, 1); LamI = f32([128, G], 1, 1)
            TH = f32([128, G], 1, 1)
            TAU = f32([128, 17], 1, 1)
            A1 = f32([128, G * 17], 1, 1); A2 = f32([128, G * 17], 1, 1); A3 = f32([128, G * 17], 1, 1)
            t32 = [f32([128, G], 1, 1) for _ in range(8)]
            T1 = f32([128, G * 16], 1, 1); T2 = f32([128, G * 16], 1, 1); T3 = f32([128, G * 16], 1, 1)
            P.op("pool", lambda h: h.memset(W1[:, 0:128], 1.0), writes=["W1"])
            P.op("pool", lambda h: h.affine_select(out=ID, in_=W1[:, 0:128], pattern=[[-1, 128]],
                                                   compare_op=ALU.is_equal, fill=0.0, base=0, channel_multiplier=1),
                 reads=["W1"], writes=["ID"])
            P.op("dve", lambda h: h.tensor_copy(out=IDb, in_=ID), reads=["ID"], writes=["IDb"])
            P.op("dve", lambda h: h.memset(HALFPI, math.pi / 2), writes=["HALFPI"])
            P.op("dve", lambda h: h.memset(EPS, LN_EPS), writes=["EPS"])
            P.op("pool", lambda h: h.iota(IOT[:], [[0, 1]], base=0, channel_multiplier=1), writes=["IOT"])
            P.op("pool", lambda h: h.iota(IOTF[:], [[1, 128]], base=0, channel_multiplier=0), writes=["IOTF"])
            P.op("dve", lambda h: h.tensor_copy(out=CIDX, in_=IOTF[:]), reads=["IOTF"], writes=["CIDX"])
            P.op("dve", lambda h: h.tensor_copy(out=SGN, in_=IOT[:]), reads=["IOT"], writes=["SGN"])
            P.op("dve", lambda h: h.tensor_scalar(out=SGN, in0=SGN, scalar1=63.5, scalar2=None, op0=ALU.is_gt),
                 reads=["SGN"], writes=["SGN"])
            P.op("dve", lambda h: h.tensor_scalar(out=SGN, in0=SGN, scalar1=2.0, scalar2=-1.0, op0=ALU.mult, op1=ALU.add),
                 reads=["SGN"], writes=["SGN"])
            P.op("dve", lambda h: h.tensor_scalar(out=NSGN, in0=SGN, scalar1=-1.0, scalar2=None, op0=ALU.mult),
                 reads=["SGN"], writes=["NSGN"])
            P.op("dve", lambda h: h.tensor_copy(out=M1, in_=IOT[:]), reads=["IOT"], writes=["M1"])
            P.op("dve", lambda h: h.tensor_scalar(out=M1, in0=M1, scalar1=1.0 / 16, scalar2=-0.46875, op0=ALU.mult, op1=ALU.add),
                 reads=["M1"], writes=["M1"])
            P.op("dve", lambda h: h.tensor_scalar(out=M1, in0=M1, scalar1=MAGIC, scalar2=-MAGIC, op0=ALU.add, op1=ALU.add),
                 reads=["M1"], writes=["M1"])
            P.op("dve", lambda h: h.tensor_scalar(out=M0, in0=M1, scalar1=0.5, scalar2=-0.25, op0=ALU.mult, op1=ALU.add),
                 reads=["M1"], writes=["M0"])
            P.op("dve", lambda h: h.tensor_scalar(out=M0, in0=M0, scalar1=MAGIC, scalar2=-MAGIC, op0=ALU.add, op1=ALU.add),
                 reads=["M0"], writes=["M0"])
            P.op("dve", lambda h: h.scalar_tensor_tensor(out=M1, in0=M0, scalar=-2.0, in1=M1, op0=ALU.mult, op1=ALU.add),
                 reads=["M0", "M1"], writes=["M1"])
            P.op("dve", lambda h: h.tensor_scalar(out=M0, in0=M1, scalar1=-1.0, scalar2=1.0, op0=ALU.mult, op1=ALU.add),
                 reads=["M1"], writes=["M0"])
            P.op("dve", lambda h: h.tensor_scalar(out=INVC, in0=CIDX[:, 0:16], scalar1=1.0, scalar2=None, op0=ALU.add),
                 reads=["CIDX"], writes=["INVC"])
            P.op("dve", lambda h: h.reciprocal(out=INVC, in_=INVC), reads=["INVC"], writes=["INVC"])
            def weight_chain():
                for cb in range(4):
                    for kh in range(2):
                        srcw = w_in[kh * 512:(kh + 1) * 512, cb * 512:(cb + 1) * 512].rearrange("(k p) f -> p k f", p=128)
                        dstw = Win3[:, kh * 4:(kh + 1) * 4, cb * 512:(cb + 1) * 512]
                        prev_ = ["Win_%d_0" % (cb - 1), "Win_%d_1" % (cb - 1)] if cb > 0 else []
                        P.dma(dstw, srcw, reads=prev_, writes=["Win_%d_%d" % (cb, kh)], key="win%d" % cb, queue="pool")
                P.dma(Wpool3, pool_w.rearrange("g c d -> c g d"), writes=["Wpool"], key="wpool", queue="pool")

            def param_head():
                bt = nb()

                def trl2(h, bt=bt):
                    h.transpose(out=PS[bt][:, 0:G], in_=L2[0:G, 0:128], identity=ID[0:G, 0:G])
                    return h.transpose(out=PS[bt][:, G:2 * G], in_=L2[0:G, 128:256], identity=ID[0:G, 0:G])
                P.op("pe", trl2, reads=["L2", "ID"], writes=["ps%d" % bt])
                P.op("dve", lambda h, bt=bt: h.tensor_copy(out=LamR, in_=PS[bt][:, 0:G]), reads=["ps%d" % bt], writes=["LamR"])
                P.op("dve", lambda h, bt=bt: h.tensor_copy(out=LamI, in_=PS[bt][:, G:2 * G]), reads=["ps%d" % bt], writes=["LamI"])

            def param_chain():
                P.op("act", lambda h: h.activation(out=Dt, in_=Dt, func=AF.Exp), reads=["Dt"], writes=["Dt"])
                P.op("dve", lambda h: h.tensor_tensor(out=TH, in0=LamI, in1=Dt, op=ALU.mult), reads=["LamI", "Dt"], writes=["TH"])
                P.op("dve", lambda h: h.tensor_tensor(out=LR, in0=LamR, in1=Dt, op=ALU.mult), reads=["LamR", "Dt"], writes=["LR"])
                reduce_angle("dve", THr, TH, t32[0], "THr", "TH", "t0")
                P.op("dve", lambda h: h.tensor_copy(out=TAU, in_=CIDX[:, 0:17]), reads=["CIDX"], writes=["TAU"])
                thr_b = raw(THr, [[1, G], [0, 17]])
                lr_b = raw(LR, [[1, G], [0, 17]])
                tau_b = raw(TAU, [[0, G], [1, 17]])
                A1_3 = A1.rearrange("p (g t) -> p g t", g=G)
                A2_3 = A2.rearrange("p (g t) -> p g t", g=G)
                A3_3 = A3.rearrange("p (g t) -> p g t", g=G)
                P.op("dve", lambda h: h.tensor_tensor(out=A1_3, in0=thr_b, in1=tau_b, op=ALU.mult), reads=["THr", "TAU"], writes=["A1"])
                reduce_angle("dve", A2, A1, A3, "A2", "A1", "A3")
                sincos(PI, PR, A2, A3, "PI", "PR", "A2", "A3")
                P.op("dve", lambda h: h.tensor_tensor(out=A1_3, in0=lr_b, in1=tau_b, op=ALU.mult), reads=["LR", "TAU", "A2"], writes=["A1"])
                P.op("act", lambda h: h.activation(out=A1, in_=A1, func=AF.Exp), reads=["A1"], writes=["A1"])
                P.op("dve", lambda h: h.tensor_tensor(out=PR, in0=PR, in1=A1, op=ALU.mult), reads=["PR", "A1"], writes=["PR"])
                P.op("dve", lambda h: h.tensor_tensor(out=PI, in0=PI, in1=A1, op=ALU.mult), reads=["PI", "A1"], writes=["PI"])
                P.op("dve", lambda h: h.tensor_scalar(out=PIs, in0=PI, scalar1=SGN[:, 0:1], scalar2=None, op0=ALU.mult),
                     reads=["PI", "SGN"], writes=["PIs"])
                ar = PR3[:, :, 1]; ai = PI3[:, :, 1]
                den, rden, nr, qr, qi, tq = t32[1], t32[2], t32[3], t32[4], t32[5], t32[6]
                P.op("dve", lambda h: h.tensor_tensor(out=den, in0=LamR, in1=LamR, op=ALU.mult), reads=["LamR"], writes=["den"])
                P.op("dve", lambda h: h.tensor_tensor(out=tq, in0=LamI, in1=LamI, op=ALU.mult), reads=["LamI"], writes=["tq"])
                P.op("dve", lambda h: h.tensor_tensor(out=den, in0=den, in1=tq, op=ALU.add), reads=["den", "tq"], writes=["den"])
                P.op("dve", lambda h: h.reciprocal(out=rden, in_=den), reads=["den"], writes=["rden"])
                P.op("dve", lambda h: h.tensor_scalar(out=nr, in0=ar, scalar1=-1.0, scalar2=None, op0=ALU.add), reads=["PR"], writes=["nr"])
                P.op("dve", lambda h: h.tensor_tensor(out=qr, in0=nr, in1=LamR, op=ALU.mult), reads=["nr", "LamR"], writes=["qr"])
                P.op("dve", lambda h: h.tensor_tensor(out=tq, in0=ai, in1=LamI, op=ALU.mult), reads=["PI", "LamI", "den"], writes=["tq"])
                P.op("dve", lambda h: h.tensor_tensor(out=qr, in0=qr, in1=tq, op=ALU.add), reads=["qr", "tq"], writes=["qr"])
                P.op("dve", lambda h: h.tensor_tensor(out=qr, in0=qr, in1=rden, op=ALU.mult), reads=["qr", "rden"], writes=["qr"])
                P.op("dve", lambda h: h.tensor_tensor(out=qi, in0=ai, in1=LamR, op=ALU.mult), reads=["PI", "LamR"], writes=["qi"])
                P.op("dve", lambda h: h.tensor_tensor(out=tq, in0=nr, in1=LamI, op=ALU.mult), reads=["nr", "LamI", "qr"], writes=["tq"])
                P.op("dve", lambda h: h.tensor_tensor(out=qi, in0=qi, in1=tq, op=ALU.subtract), reads=["qi", "tq"], writes=["qi"])
                P.op("dve", lambda h: h.tensor_tensor(out=qi, in0=qi, in1=rden, op=ALU.mult), reads=["qi", "rden"], writes=["qi"])
                P.op("act", lambda h: h.activation(out=Bcat[64:128, :], in_=Bcat[0:64, :], func=AF.Copy), reads=["Bcat"], writes=["Bcat"])
                bre = Bcat3[:, :, 0:16]; bim = Bcat3[:, :, 16:32]
                qr_b = raw(qr, [[1, G], [0, 16]]); qi_b = raw(qi, [[1, G], [0, 16]])
                T1_3 = T1.rearrange("p (g c) -> p g c", g=G); T2_3 = T2.rearrange("p (g c) -> p g c", g=G)
                T3_3 = T3.rearrange("p (g c) -> p g c", g=G)
                P.op("dve", lambda h: h.tensor_tensor(out=T1_3, in0=bre, in1=qr_b, op=ALU.mult), reads=["Bcat", "qr"], writes=["T1"])
                P.op("dve", lambda h: h.tensor_tensor(out=T3_3, in0=bim, in1=qi_b, op=ALU.mult), reads=["Bcat", "qi"], writes=["T3"])
                P.op("dve", lambda h: h.tensor_tensor(out=T1, in0=T1, in1=T3, op=ALU.subtract), reads=["T1", "T3"], writes=["T1"])
                P.op("dve", lambda h: h.tensor_tensor(out=T2_3, in0=bim, in1=qr_b, op=ALU.mult), reads=["Bcat", "qr"], writes=["T2"])
                P.op("dve", lambda h: h.tensor_tensor(out=T3_3, in0=bre, in1=qi_b, op=ALU.mult), reads=["Bcat", "qi", "T1"], writes=["T3"])
                P.op("dve", lambda h: h.tensor_tensor(out=T2, in0=T2, in1=T3, op=ALU.add), reads=["T2", "T3"], writes=["T2"])
                P.op("act", lambda h: h.activation(out=BB[0:64, :], in_=T1[0:64, :], func=AF.Copy), reads=["T1"], writes=["BB"])
                P.op("act", lambda h: h.activation(out=BB[64:128, :], in_=T2[64:128, :], func=AF.Copy), reads=["T2"], writes=["BB"])
                P.op("act", lambda h: h.activation(out=BBsw[0:64, :], in_=T2[0:64, :], func=AF.Copy), reads=["T2"], writes=["BBsw"])
                P.op("act", lambda h: h.activation(out=BBsw[64:128, :], in_=T1[64:128, :], func=AF.Copy), reads=["T1"], writes=["BBsw"])


            tiles = [(0, 512), (512, 512), (1024, 512), (1536, 512), (2048, NS)]
            xt_i = [0]
            subsA = []
            for (t0_, nt_) in tiles:
                if nt_ == NS:
                    subsA.append((xs, NS))
                else:
                    for sidx_ in range(4):
                        subsA.append((xp[t0_ + sidx_ * 128: t0_ + (sidx_ + 1) * 128, :], 128))

            def load_xt(k):
                if k >= len(subsA):
                    return
                src_, ns_ = subsA[k]
                P.dma(Xt[k % 2][0:ns_], src_, writes=["Xt%d" % (k % 2)], key="xt%d" % (k % 2))

            def xT_part(ti, t0, nt):
                is_s = (nt == NS)
                nsub = 1 if is_s else 4
                xT3 = xTd[ti % 2]
                xtn = "xT%d" % (ti % 2)
                for sidx in range(nsub):
                    ns = NS if is_s else 128
                    xs_slot = xt_i[0] % 2
                    xt_i[0] += 1
                    for half in range(2):
                        b = nb()

                        def tr(h, b=b, half=half, ns=ns, xs_slot=xs_slot):
                            r = None
                            for j in range(4):
                                kc = half * 4 + j
                                r = h.transpose(out=PS[b][:, j * 128: j * 128 + ns],
                                                in_=Xt[xs_slot][0:ns, kc * 128:(kc + 1) * 128],
                                                identity=ID[0:ns, 0:ns])
                            return r
                        P.op("pe", tr, reads=["Xt%d" % xs_slot, "ID"], writes=["ps%d" % b])
                        src_ps = PS[b][:, :].rearrange("p (j t) -> p j t", j=4)[:, :, 0:ns]
                        dst = xT3[:, half * 4:(half + 1) * 4, sidx * 128: sidx * 128 + ns]
                        if half == 0:
                            P.op("act", lambda h, dst=dst, src_ps=src_ps: h.activation(out=dst, in_=src_ps, func=AF.Copy),
                                 reads=["ps%d" % b], writes=[xtn])
                        else:
                            P.op("dve", lambda h, dst=dst, src_ps=src_ps: h.tensor_copy(out=dst, in_=src_ps),
                                 reads=["ps%d" % b], writes=[xtn])

                    load_xt(xt_i[0] + 1)

            def tileA(ti, t0, nt, with_xt=False, nxt=None):
                is_s = (nt == NS)
                if with_xt:
                    xT_part(ti, t0, nt)
                xT3 = xTd[ti % 2]
                xtn = "xT%d" % (ti % 2)

                def proj(ot, b):
                    def f(h):
                        r = None
                        for kc in range(8):
                            r = h.matmul(PS[b][:, 0:nt], lhsT=Win3[:, kc, ot * 128:(ot + 1) * 128],
                                         rhs=xT3[:, kc, 0:nt], start=(kc == 0), stop=(kc == 7))
                        return r
                    P.op("pe", f, reads=[xtn, "Win_%d_0" % (ot // 4), "Win_%d_1" % (ot // 4)], writes=["ps%d" % b])

                for q in range(4):
                    b = nb()
                    proj(q, b)
                    P.op("dve", lambda h, b=b, q=q: h.tensor_copy(out=uT3[:, q, t0:t0 + nt], in_=PS[b][:, 0:nt]),
                         reads=["ps%d" % b], writes=["uT"])
                for q in range(4):
                    b = nb()
                    proj(4 + q, b)
                    P.op("act", lambda h, b=b, q=q: h.activation(out=SGt[ti % 2][:, q, 0:nt], in_=PS[b][:, 0:nt], func=AF.Silu),
                         reads=["ps%d" % b], writes=["SGt%d" % (ti % 2)])
                P.dma(SGscr[:, :, t0:t0 + nt], SGt[ti % 2][:, :, 0:nt], reads=["SGt%d" % (ti % 2)], key="spillsg%d" % (ti % 2), queue="act")
                if nxt is not None:
                    xT_part(*nxt)

                def poolg(gp):
                    w = POOL_WINDOWS[gp]
                    sl = gp % 2
                    b = nb()
                    proj(8 + gp, b)
                    if not is_s:
                        P.op("act", lambda h, b=b, sl=sl: h.activation(out=PinT[sl][:, 0:nt], in_=PS[b][:, 0:nt], func=AF.Copy),
                             reads=["ps%d" % b], writes=["PinT%d" % sl])
                        if ti == 3:
                            P.op("dve", lambda h, b=b, gp=gp: h.tensor_copy(out=PLAST3[:, gp, :], in_=PS[b][:, nt - 16:nt]),
                                 reads=["ps%d" % b], writes=["PLAST"])
                    else:
                        P.op("dve", lambda h, b=b, gp=gp: h.tensor_copy(out=PinS3[:, gp, :], in_=PS[b][:, 0:nt]),
                             reads=["ps%d" % b], writes=["PinS"])
                    b2 = nb()
                    proj(12 + gp, b2)
                    P.op("act", lambda h, b2=b2, sl=sl: h.activation(out=SPG[sl][:, 0:nt], in_=PS[b2][:, 0:nt], func=AF.Silu),
                         reads=["ps%d" % b2], writes=["SPG%d" % sl])
                    if not is_s:
                        if ti == 0:
                            P.op("pool", lambda h, gp=gp: h.memset(V[gp][:, 0:15], 0.0), writes=["V%d" % gp])
                        else:
                            P.op("pool", lambda h, gp=gp: h.tensor_copy(out=V[gp][:, 0:15], in_=V[gp][:, 512:527]),
                                 reads=["V%d" % gp], writes=["V%d" % gp])
                        b3 = nb()
                        P.op("pe", lambda h, b3=b3, gp=gp, sl=sl: h.matmul(PS[b3][:, 0:nt], lhsT=Wpool3[:, gp, :], rhs=PinT[sl][:, 0:nt],
                                                                         start=True, stop=True),
                             reads=["PinT%d" % sl, "Wpool"], writes=["ps%d" % b3])
                        P.op("act", lambda h, b3=b3, gp=gp: h.activation(out=V[gp][:, 15:15 + nt], in_=PS[b3][:, 0:nt], func=AF.Copy),
                             reads=["ps%d" % b3], writes=["V%d" % gp])
                        L = 15 + nt
                        src = V[gp]
                        srcn = "V%d" % gp
                        bufs = [(W1, "W1"), (W2, "W2")]
                        nsteps = int(math.log2(w))
                        for k in range(nsteps):
                            sh = 1 << k
                            lo = (1 << (k + 1)) - 1
                            dstb, dstn = bufs[k % 2]
                            P.op("pool", lambda h, dstb=dstb, src=src, lo=lo, sh=sh, L=L:
                                 h.tensor_tensor(out=dstb[:, lo:L], in0=src[:, lo:L], in1=src[:, lo - sh:L - sh], op=ALU.add),
                                 reads=[srcn], writes=[dstn])
                            src, srcn = dstb, dstn
                        tmp = TMP[sl]
                        P.op("dve", lambda h, tmp=tmp, src=src, gp=gp, w=w: h.scalar_tensor_tensor(
                            out=tmp[:, 0:nt], in0=src[:, 15:15 + nt], scalar=1.0 / w, in1=V[gp][:, 15:15 + nt],
                            op0=ALU.mult, op1=ALU.subtract), reads=[srcn, "V%d" % gp], writes=["TMP%d" % sl])
                        if ti == 0:
                            nfix = w - 1
                            P.op("dve", lambda h, tmp=tmp, src=src, nfix=nfix: h.tensor_tensor(
                                out=tmp[:, 0:nfix], in0=src[:, 15:15 + nfix], in1=INVC[:, 0:nfix], op=ALU.mult),
                                reads=[srcn, "INVC", "TMP%d" % sl], writes=["TMP%d" % sl])
                            P.op("dve", lambda h, tmp=tmp, gp=gp, nfix=nfix: h.tensor_tensor(
                                out=tmp[:, 0:nfix], in0=tmp[:, 0:nfix], in1=V[gp][:, 15:15 + nfix], op=ALU.subtract),
                                reads=["V%d" % gp, "TMP%d" % sl], writes=["TMP%d" % sl])
                        P.op("dve", lambda h, tmp=tmp, gp=gp, sl=sl: h.scalar_tensor_tensor(
                            out=PMt[ti % 2][:, gp, 0:nt], in0=tmp[:, 0:nt], scalar=PSC[:, gp:gp + 1], in1=SPG[sl][:, 0:nt],
                            op0=ALU.mult, op1=ALU.mult), reads=["TMP%d" % sl, "SPG%d" % sl, "PSC"], writes=["PMt%d" % (ti % 2)])
                    else:
                        if gp == 0:
                            for hh in range(2):
                                bt = nb()

                                def trp(h, bt=bt, hh=hh):
                                    r = None
                                    for g2 in range(4):
                                        r = h.transpose(out=PS[bt][:, g2 * 128: g2 * 128 + 120],
                                                        in_=SPn3[0:120, hh, g2 * 128:(g2 + 1) * 128], identity=ID[0:120, 0:120])
                                    return r
                                P.op("pe", trp, reads=["SPn", "ID"], writes=["ps%d" % bt])
                                P.op("act", lambda h, bt=bt, hh=hh: h.activation(
                                    out=PfT3[:, :, hh * 120:(hh + 1) * 120],
                                    in_=PS[bt][:, :].rearrange("p (g t) -> p g t", g=4)[:, :, 0:120], func=AF.Copy),
                                    reads=["ps%d" % bt], writes=["PfT"])
                        pf = PfT3[:, gp, :].rearrange("p (b r) -> p b r", r=15)[:, :, 16 - w:15]
                        P.op("dve", lambda h, pf=pf: h.tensor_reduce(out=WSUM, in_=pf, axis=AX.X, op=ALU.add),
                             reads=["PfT"], writes=["WSUM"])
                        P.op("dve", lambda h, gp=gp: h.tensor_tensor(out=WSUM, in0=WSUM, in1=PinS3[:, gp, :], op=ALU.add),
                             reads=["PinS", "WSUM"], writes=["WSUM"])
                        P.op("dve", lambda h, gp=gp, w=w: h.scalar_tensor_tensor(
                            out=PoS, in0=WSUM, scalar=1.0 / w, in1=PinS3[:, gp, :], op0=ALU.mult, op1=ALU.subtract),
                            reads=["WSUM", "PinS"], writes=["PoS"])
                        b3 = nb()
                        P.op("pe", lambda h, b3=b3, gp=gp: h.matmul(PS[b3][:, 0:NS], lhsT=Wpool3[:, gp, :], rhs=PoS,
                                                                  start=True, stop=True),
                             reads=["PoS", "Wpool"], writes=["ps%d" % b3])
                        P.op("dve", lambda h, b3=b3, gp=gp, sl=sl: h.scalar_tensor_tensor(
                            out=PMt[ti % 2][:, gp, 0:nt], in0=PS[b3][:, 0:NS], scalar=PSC[:, gp:gp + 1], in1=SPG[sl][:, 0:nt],
                            op0=ALU.mult, op1=ALU.mult), reads=["ps%d" % b3, "SPG%d" % sl, "PSC"], writes=["PMt%d" % (ti % 2)])

                for gp_ in range(4):
                    poolg(gp_)
                P.dma(PMscr[:, :, t0:t0 + nt], PMt[ti % 2][:, :, 0:nt], reads=["PMt%d" % (ti % 2)], key="spillpm%d" % (ti % 2), queue="pool")

            load_xt(0)
            load_xt(1)
            P.dma(PSC, pool_scale.rearrange("(q p) -> p q", p=128), writes=["PSC"], key="vec", allow_slow_non_contiguous=True)

            for j, srcd in enumerate([lam_re, lam_re, lam_im, lam_im]):
                P.dma(L2[0:G, j * 64:(j + 1) * 64], srcd, writes=["L2"], key="l2")
            P.dma(Dt, log_dt.partition_broadcast(128), writes=["Dt"], key="l2")
            P.replay([P.record(weight_chain), P.record(tileA, 0, tiles[0][0], tiles[0][1], True, (1,) + tiles[1])])
            P.dma(Bcat3[0:64, :, 0:16], b_re.rearrange("g n c -> n g c"), writes=["Bcat"], key="bc", allow_slow_non_contiguous=True)
            P.dma(Bcat3[0:64, :, 16:32], b_im.rearrange("g n c -> n g c"), writes=["Bcat"], key="bc", allow_slow_non_contiguous=True)
            P.dma(Dv, ssm_d.rearrange("(q p) -> p q", p=128), writes=["Dv"], key="vec", allow_slow_non_contiguous=True)
            P.dma(GB, glu_b.rearrange("(q p) -> p q", p=128), writes=["GB"], key="vec", allow_slow_non_contiguous=True)
            P.dma(SPn3[0:120], spool.rearrange("b r c -> (b r) c").rearrange("(h x) c -> x h c", h=2),
                  writes=["SPn"], key="spn")
            P.dma(spool_o[:, 0:14, :], spool[:, 1:15, :], key="outp")
            param_head()
            tileA(1, tiles[1][0], tiles[1][1], False, (2,) + tiles[2])
            P.replay([P.record(param_chain), P.record(tileA, 2, tiles[2][0], tiles[2][1], False, (3,) + tiles[3])])
            tileA(3, tiles[3][0], tiles[3][1], False, (4,) + tiles[4])
            tileA(4, tiles[4][0], tiles[4][1], False, None)

            for which, (src3, srcn) in enumerate([(PLAST3, "PLAST"), (PinS3, "PinS")]):
                bt = nb()

                def trl(h, bt=bt, src3=src3):
                    r = None
                    for g2 in range(4):
                        r = h.transpose(out=PS[bt][0:16, g2 * 128:(g2 + 1) * 128], in_=src3[:, g2, :], identity=ID)
                    return r
                P.op("pe", trl, reads=[srcn, "ID"], writes=["ps%d" % bt])
                P.op("act", lambda h, bt=bt: h.activation(out=ROW[0:16, :], in_=PS[bt][0:16, :], func=AF.Copy),
                     reads=["ps%d" % bt], writes=["ROW"])
                if which == 0:
                    P.dma(ppool[0:15, :], ROW[1:16, :], reads=["ROW"], key="outp")
                else:
                    P.dma(spool_o[:, 14, :], ROW[0:16, :], reads=["ROW"], key="outp")
            P.barrier()
            P.emit(blk)
        if KSTOP == 1:
            return nc

        CAqball = [b16([128, 8 * 17 * 16], 2, 3) for _ in range(4)]
        CAp0 = b16([128, G * 32], 2, 3)
        CAp03 = CAp0.rearrange("p (g m) -> p g m", g=G)
        H0n = f32([128, 4 * 128], 3, 3); H0n3 = H0n.rearrange("p (q n) -> p q n", q=4)
        H0n2 = f32([128, 4 * 128], 3, 3); H0n23 = H0n2.rearrange("p (q n) -> p q n", q=4)
        with ExitStack() as s2:
            blk = s2.enter_context(nc.Block())
            BAqd = [f32([128, 16 * 128], 2, 2) for _ in range(2)]; BAtd = [f32([128, 16 * 128], 2, 2) for _ in range(2)]
            Cnat = f32([128, 4 * 128], 2, 2); Cnat2 = f32([128, 4 * 128], 2, 2)
            Cnat3 = Cnat.rearrange("p (q n) -> p q n", q=4); Cnat23 = Cnat2.rearrange("p (q n) -> p q n", q=4)
            CC = f32([128, 4 * 128], 2, 2); CCsw = f32([128, 4 * 128], 2, 2)
            CAq = f32([128, 8 * 17 * 16], 2, 2); CAt = f32([128, 8 * 17 * 16], 2, 2)
            CAq4 = CAq.rearrange("p (g t c) -> p g t c", g=8, t=17); CAt4 = CAt.rearrange("p (g t c) -> p g t c", g=8, t=17)
            BApT = b16([128, 4 * 16 * 2 * 128], 2, 2)
            BAp5 = BApT.rearrange("p (q t e n) -> p q t e n", q=4, t=16, e=2)

            if KSTOP == 2 and KSUB == 2:
                P.barrier(); P.emit(blk); return nc
            def ba_chain(q):
                BAq = BAqd[q % 2]; BAt = BAtd[q % 2]
                nq = "BAq%d" % (q % 2); nt_ = "BAt%d" % (q % 2)
                BAq4 = BAq.rearrange("p (t g c) -> p t g c", t=16, g=8)
                BAt4 = BAt.rearrange("p (t g c) -> p t g c", t=16, g=8)
                pr_b = raw(PR, [[1, 16], [17, 8], [0, 16]], extra_off=q * 8 * 17)
                pis_b = raw(PIs, [[1, 16], [17, 8], [0, 16]], extra_off=q * 8 * 17)
                bb_b = raw(BB, [[0, 16], [16, 8], [1, 16]], extra_off=q * 8 * 16)
                bbsw_b = raw(BBsw, [[0, 16], [16, 8], [1, 16]], extra_off=q * 8 * 16)
                P.op("dve", lambda h, pr_b=pr_b, bb_b=bb_b, BAq4=BAq4: h.tensor_tensor(out=BAq4, in0=pr_b, in1=bb_b, op=ALU.mult),
                     reads=["PR", "BB"], writes=[nq])
                P.op("pool", lambda h, pis_b=pis_b, bbsw_b=bbsw_b, BAt4=BAt4: h.tensor_tensor(out=BAt4, in0=pis_b, in1=bbsw_b, op=ALU.mult),
                     reads=["PIs", "BBsw"], writes=[nt_])
                P.op("dve", lambda h, BAq=BAq, BAt=BAt: h.tensor_tensor(out=BAq, in0=BAq, in1=BAt, op=ALU.add), reads=[nq, nt_], writes=[nq])
                for tq4 in range(4):
                    bt = nb()

                    def trb(h, bt=bt, tq4=tq4, BAq=BAq):
                        r = None
                        for j in range(4):
                            r = h.transpose(out=PS[bt][:, j * 128:(j + 1) * 128], in_=BAq[:, (tq4 * 4 + j) * 128:(tq4 * 4 + j + 1) * 128],
                                            identity=ID)
                        return r
                    P.op("pe", trb, reads=[nq, "ID"], writes=["ps%d" % bt])
                    psv = PS[bt][:, :].rearrange("p (t n) -> p t n", t=4)
                    P.op("act", lambda h, psv=psv, q=q, tq4=tq4: h.activation(
                        out=BAp5[:, q, tq4 * 4:(tq4 + 1) * 4, 0, :], in_=psv, func=AF.Copy, scale=M0[:, 0:1]),
                        reads=["ps%d" % bt, "M0"], writes=["BAp%d" % q])
                    P.op("act", lambda h, psv=psv, q=q, tq4=tq4: h.activation(
                        out=BAp5[:, q, tq4 * 4:(tq4 + 1) * 4, 1, :], in_=psv, func=AF.Copy, scale=M1[:, 0:1]),
                        reads=["ps%d" % bt, "M1"], writes=["BAp%d" % q])

            P.replay([P.record(ba_chain, 0), P.record(ba_chain, 1)])
            P.replay([P.record(ba_chain, 2), P.record(ba_chain, 3)])
            P.op("pool", lambda h: h.memset(CAp0, 0.0), writes=["CAp0"])
            P.dma(Cnat3[:, :, 0:64], c_re.rearrange("(q gl) co n -> (gl co) q n", q=4), writes=["Cnat"], key="cn")
            P.dma(Cnat3[:, :, 64:128], c_im.rearrange("(q gl) co n -> (gl co) q n", q=4), writes=["Cnat"], key="cn")
            P.dma(Cnat23[:, :, 0:64], c_im.rearrange("(q gl) co n -> (gl co) q n", q=4), writes=["Cnat2"], key="cn")
            P.dma(Cnat23[:, :, 64:128], c_re.rearrange("(q gl) co n -> (gl co) q n", q=4), writes=["Cnat2"], key="cn")
            for (srcC, dstC, nm_s, nm_d, sgn) in [(Cnat3, CC, "Cnat", "CC", True), (Cnat23, CCsw, "Cnat2", "CCsw", False)]:
                bt = nb()

                def trc(h, bt=bt, srcC=srcC):
                    r = None
                    for q in range(4):
                        r = h.transpose(out=PS[bt][:, q * 128:(q + 1) * 128], in_=srcC[:, q, :], identity=ID)
                    return r
                P.op("pe", trc, reads=[nm_s, "ID"], writes=["ps%d" % bt])
                if sgn:
                    P.op("dve", lambda h, bt=bt, dstC=dstC: h.tensor_scalar(out=dstC, in0=PS[bt][:, :], scalar1=NSGN[:, 0:1], scalar2=None, op0=ALU.mult),
                         reads=["ps%d" % bt, "NSGN"], writes=[nm_d])
                else:
                    P.op("dve", lambda h, bt=bt, dstC=dstC: h.tensor_copy(out=dstC, in_=PS[bt][:, :]), reads=["ps%d" % bt], writes=[nm_d])
            def c1a_chain(q):
                g0 = q * 8
                pr_b = raw(PR, [[17, 8], [1, 17], [0, 16]], extra_off=g0 * 17)
                pi_b = raw(PI, [[17, 8], [1, 17], [0, 16]], extra_off=g0 * 17)
                cc_b = raw(CC, [[16, 8], [0, 17], [1, 16]], extra_off=q * 128)
                ccsw_b = raw(CCsw, [[16, 8], [0, 17], [1, 16]], extra_off=q * 128)
                P.op("dve", lambda h, pr_b=pr_b, cc_b=cc_b: h.tensor_tensor(out=CAq4, in0=pr_b, in1=cc_b, op=ALU.mult),
                     reads=["PR", "CC", "CAq"], writes=["CAq"])
                P.op("pool", lambda h, pi_b=pi_b, ccsw_b=ccsw_b: h.tensor_tensor(out=CAt4, in0=pi_b, in1=ccsw_b, op=ALU.mult),
                     reads=["PI", "CCsw"], writes=["CAt"])
                P.op("dve", lambda h: h.tensor_tensor(out=CAq, in0=CAq, in1=CAt, op=ALU.subtract), reads=["CAq", "CAt"], writes=["CAq"])
                P.op("act", lambda h: h.activation(out=CAqball[q], in_=CAq, func=AF.Copy), reads=["CAq"], writes=["CAqb%d" % q])
                for e in range(2):
                    co = raw(CAp0, [[64, 4], [1, 16]], extra_off=(g0 + e) * 32 + e * 16)
                    ci = raw(CAq, [[2 * 272, 4], [1, 16]], extra_off=e * 272)
                    P.op("act", lambda h, co=co, ci=ci: h.activation(out=co, in_=ci, func=AF.Copy), reads=["CAq"], writes=["CAp0"])
            for q_ in range(4):
                c1a_chain(q_)
            if KSTOP == 2 and KSUB == 3:
                P.barrier(); P.emit(blk); return nc
            uTc = uT3[:, :, 0:SEQ].rearrange("p q (c i) -> p q c i", i=TCH)
            sbanks = [[nb(), nb()] for _ in range(4)]
            allb = ["ps%d" % x for pair in sbanks for x in pair]

            def smm(h):
                r = None
                for q in range(4):
                    for e in range(2):
                        k = (q % 2) * 2 + e
                        for i in range(TCH):
                            for rr in range(4):
                                bt = sbanks[rr][q // 2]
                                r = h.matmul(PS[bt][:, k * 128:(k + 1) * 128],
                                             lhsT=BAp5[32 * rr:32 * rr + 32, q, 15 - i, e, :],
                                             rhs=uTc[32 * rr:32 * rr + 32, q, :, i],
                                             start=(i == 0), stop=(i == TCH - 1), tile_position=(32 * rr, 0))
                return r
            P.op("pe", smm, reads=["BAp0", "BAp1", "BAp2", "BAp3", "uT"], writes=allb)
            for rr in range(4):
                for qh in range(2):
                    bt = sbanks[rr][qh]
                    so = raw(S_sb, [[1024, 2], [128, 2], [1, 128]], extra_off=qh * 2048 + rr * 256)
                    si = PS[bt][:, :].rearrange("p (a e c) -> p a e c", a=2, e=2)
                    if (rr + qh) % 2 == 0:
                        P.op("act", lambda h, so=so, si=si: h.activation(out=so, in_=si, func=AF.Copy), reads=["ps%d" % bt], writes=["S"])
                    else:
                        P.op("dve", lambda h, so=so, si=si: h.tensor_copy(out=so, in_=si), reads=["ps%d" % bt], writes=["S"])
            if KSTOP == 2 and KSUB == 4:
                P.barrier(); P.emit(blk); return nc
            xbanks = [nb() for _ in range(4)]

            def xsm(h):
                r = None
                for q in range(4):
                    for e in range(2):
                        for rr in range(4):
                            k = q * 2 + e
                            r = h.matmul(PS[xbanks[rr]][:, k * NS:(k + 1) * NS], lhsT=BAp5[32 * rr:32 * rr + 32, q, 0, e, :],
                                         rhs=uT3[32 * rr:32 * rr + 32, q, SEQ:SEQ + NS], start=True, stop=True,
                                         tile_position=(32 * rr, 0))
                return r
            P.op("pe", xsm, reads=["BAp0", "BAp1", "BAp2", "BAp3", "uT"], writes=["ps%d" % x for x in xbanks])
            if KSTOP == 2 and KSUB == 5:
                P.barrier(); P.emit(blk); return nc
            for rr in range(4):
                xo = raw(Xs_sb, [[128, 4], [16, 2], [1, 16]], extra_off=rr * 32)
                xi = PS[xbanks[rr]][:, 0:8 * NS].rearrange("p (q e b) -> p q e b", q=4, e=2)
                P.op("dve", lambda h, xo=xo, xi=xi: h.tensor_copy(out=xo, in_=xi), reads=["ps%d" % xbanks[rr]], writes=["Xs"])
            P.barrier()
            P.emit(blk)
        if KSTOP == 2:
            return nc

        with ExitStack() as s3:
            blk = s3.enter_context(nc.Block())
            TH16 = f32([128, G], 3, 3); TH16r = f32([128, G], 3, 3); RHO = f32([128, G], 3, 3); tt0 = f32([128, G], 3, 3)
            MAGICT = f32([128, 1], 3, 3); NMAGICT = f32([128, 1], 3, 3)
            GN = 8
            NB = GN * NCH
            BfA = [f32([128, NB], 3, 3) for _ in range(6)]
            BfB = [f32([128, NB], 3, 3) for _ in range(3)] + [A16.alloc(2 * NB, 3, 3).bitcast(F32) for _ in range(3)]
            hb_ctr = [0]
            Hb = b16([128, G * 129], 3, 3)
            Hb3 = Hb.rearrange("p (g c) -> p g c", g=G)
            Hfin = f32([128, G], 3, 3)
            HfT = f32([128, 128], 3, 3)
            Bpad = b16([128, 8 * 128], 3, 3)
            Bpad3 = Bpad.rearrange("p (g m) -> p g m", g=8)
            Ddiag = f32([128, 4 * 128], 3, 3); Ddiag3 = Ddiag.rearrange("p (q m) -> p q m", q=4)
            Ddb = b16([128, 4 * 128], 3, 3); Ddb3 = Ddb.rearrange("p (q m) -> p q m", q=4)
            AIs = f32([128, G], 3, 3)
            Hn = f32([128, G * NS], 3, 3); Hn3 = Hn.rearrange("p (g b) -> p g b", g=G)
            Hn2 = f32([128, G * NS], 3, 3)
            Hnb = b16([128, G * NS], 3, 3); Hnb3 = Hnb.rearrange("p (g b) -> p g b", g=G)
            Hout = f32([128, 4 * 128], 3, 3); Hout3 = Hout.rearrange("p (q n) -> p q n", q=4)

            def reduce_angle3(dst, src, tmpb, n_dst, n_src, n_tmp, eng="dve"):
                P.op("act", lambda h: h.activation(out=tmpb, in_=src, func=AF.Identity, scale=1.0 / TWO_PI, bias=MAGICT),
                     reads=[n_src, "MAGICT"], writes=[n_tmp])
                P.op("act", lambda h: h.activation(out=tmpb, in_=tmpb, func=AF.Identity, scale=1.0, bias=NMAGICT),
                     reads=[n_tmp, "MAGICT"], writes=[n_tmp])
                P.op(eng, lambda h: h.scalar_tensor_tensor(out=dst, in0=tmpb, scalar=-TWO_PI, in1=src, op0=ALU.mult, op1=ALU.add),
                     reads=[n_tmp, n_src], writes=[n_dst])

            P.op("pool", lambda h: h.memset(MAGICT, MAGIC), writes=["MAGICT"])
            P.op("pool", lambda h: h.memset(NMAGICT, -MAGIC), writes=["MAGICT"])
            P.op("dve", lambda h: h.tensor_scalar(out=TH16, in0=THr, scalar1=float(TCH), scalar2=None, op0=ALU.mult), reads=["THr"], writes=["TH16"])
            reduce_angle3(TH16r, TH16, tt0, "TH16r", "TH16", "tt0")
            P.op("act", lambda h: h.activation(out=RHO, in_=LR, func=AF.Exp, scale=float(TCH)), reads=["LR"], writes=["RHO"])
            P.op("pool", lambda h: h.memset(Hb, 0.0), writes=["Hb%d" % (4 * i_) for i_ in range(8)])
            for q in range(4):
                P.op("pool", lambda h, q=q: h.tensor_scalar(out=Ddiag3[:, q, :], in0=ID, scalar1=Dv[:, q:q + 1], scalar2=None, op0=ALU.mult),
                     reads=["ID", "Dv"], writes=["Ddiag"])
            P.op("pool", lambda h: h.tensor_copy(out=Ddb, in_=Ddiag), reads=["Ddiag"], writes=["Ddb"])

            for gl in range(8):
                sre_v = sre.rearrange("b (q gl) n -> gl b q n", q=4)[gl]
                sim_v = sim.rearrange("b (q gl) n -> gl b q n", q=4)[gl]
                P.dma(H0n3[gl * NS:(gl + 1) * NS, :, 0:64], sre_v, writes=["H0n"], key="h0")
                P.dma(H0n3[gl * NS:(gl + 1) * NS, :, 64:128], sim_v, writes=["H0n"], key="h0")
                P.dma(H0n23[gl * NS:(gl + 1) * NS, :, 0:64], sim_v, writes=["H0n2"], key="h0")
                P.dma(H0n23[gl * NS:(gl + 1) * NS, :, 64:128], sre_v, writes=["H0n2"], key="h0")
            def hb_chain(q, hb):
                g0 = q * 8
                if True:
                    g0h = g0
                    bs = q % 2
                    Bf = BfA if bs == 0 else BfB
                    PH, PHr, SINP, COSP, SMS, SW = Bf
                    Z, ZT = Bf[0], Bf[1]
                    B0, B1, B2, B3, B4, B5 = ["B%d_%d" % (bs, i) for i in range(6)]
                    th_b = raw(TH16r, [[1, GN], [0, NCH]], extra_off=g0h)
                    ci_b = raw(CIDX, [[0, GN], [1, NCH]])
                    PH3 = PH.rearrange("p (g c) -> p g c", g=GN)
                    P.op("pool", lambda h, th_b=th_b, ci_b=ci_b, PH3=PH3: h.tensor_tensor(out=PH3, in0=th_b, in1=ci_b, op=ALU.mult),
                         reads=["TH16r", "CIDX", B0], writes=[B0])
                    reduce_angle3(PHr, PH, SW, B1, B0, B5, eng="dve")
                    P.op("act", lambda h, SINP=SINP, PHr=PHr: h.activation(out=SINP, in_=PHr, func=AF.Sin, scale=0.999999), reads=[B1], writes=[B2])
                    P.op("act", lambda h, PH=PH, PHr=PHr: h.activation(out=PH, in_=PHr, func=AF.Abs), reads=[B1], writes=[B0])
                    P.op("act", lambda h, COSP=COSP, PH=PH: h.activation(out=COSP, in_=PH, func=AF.Sin, bias=HALFPI, scale=-0.999999),
                         reads=[B0, "HALFPI"], writes=[B3])
                    P.op("pool", lambda h, SMS=SMS, SINP=SINP: h.tensor_scalar(out=SMS, in0=SINP, scalar1=NSGN[:, 0:1], scalar2=1.0, op0=ALU.mult, op1=ALU.mult),
                         reads=[B2, "NSGN"], writes=[B4])
                    Sq = S_sb[:, g0h * NCH:(g0h + GN) * NCH]
                    sn = ["S"]
                    P.op("act", lambda h, Sq=Sq, SW=SW: h.activation(out=SW[0:64, :], in_=Sq[64:128, :], func=AF.Copy), reads=sn, writes=[B5])
                    P.op("act", lambda h, Sq=Sq, SW=SW: h.activation(out=SW[64:128, :], in_=Sq[0:64, :], func=AF.Copy), reads=sn, writes=[B5])
                    P.op("dve", lambda h, Sq=Sq, Z=Z, COSP=COSP: h.tensor_tensor(out=Z, in0=Sq, in1=COSP, op=ALU.mult), reads=sn + [B3, B0], writes=[B0])
                    P.op("pool", lambda h, ZT=ZT, SW=SW, SMS=SMS: h.tensor_tensor(out=ZT, in0=SW, in1=SMS, op=ALU.mult), reads=[B5, B4, B1], writes=[B1])
                    P.op("dve", lambda h, Z=Z, ZT=ZT: h.tensor_tensor(out=Z, in0=Z, in1=ZT, op=ALU.add), reads=[B0, B1], writes=[B0])
                    for gl in range(GN):
                        rho_b = raw(RHO, [[0, NCH]], extra_off=g0h + gl)
                        P.op("dve", lambda h, gl=gl, rho_b=rho_b, Z=Z, ZT=ZT: h.tensor_tensor_scan(
                            out=ZT[:, gl * NCH:(gl + 1) * NCH], data0=rho_b, data1=Z[:, gl * NCH:(gl + 1) * NCH],
                            initial=0.0, op0=ALU.mult, op1=ALU.add), reads=[B0, "RHO", B1], writes=[B1])
                    P.op("act", lambda h, SW=SW, ZT=ZT: h.activation(out=SW[0:64, :], in_=ZT[64:128, :], func=AF.Copy), reads=[B1, B5], writes=[B5])
                    P.op("act", lambda h, SW=SW, ZT=ZT: h.activation(out=SW[64:128, :], in_=ZT[0:64, :], func=AF.Copy), reads=[B1, B5], writes=[B5])
                    P.op("dve", lambda h, Z=Z, ZT=ZT, COSP=COSP: h.tensor_tensor(out=Z, in0=ZT, in1=COSP, op=ALU.mult), reads=[B1, B3, B0], writes=[B0])
                    P.op("pool", lambda h, ZT=ZT, SW=SW, SMS=SMS: h.tensor_tensor(out=ZT, in0=SW, in1=SMS, op=ALU.mult), reads=[B5, B4, B0, B1], writes=[B1])
                    P.op("dve", lambda h, Z=Z, ZT=ZT: h.tensor_tensor(out=Z, in0=Z, in1=ZT, op=ALU.subtract), reads=[B0, B1], writes=[B0])
                    Z3 = Z.rearrange("p (g c) -> p g c", g=GN)
                    P.op("act", lambda h, g0h=g0h, Z3=Z3: h.activation(out=Hb3[:, g0h:g0h + GN, 1:129], in_=Z3, func=AF.Copy), reads=[B0], writes=["Hb%d" % g0h, "Hb%d" % (g0h + 4)])
                    P.op("pool", lambda h, g0h=g0h, Z3=Z3: h.tensor_copy(out=Hfin[:, g0h:g0h + GN], in_=Z3[:, :, NCH - 1]), reads=[B0], writes=["Hfin%d" % g0h, "Hfin%d" % (g0h + 4)])

            def c1b_chain(q):
                g0 = q * 8
                if q == 0:
                    P.op("pool", lambda h: h.memset(Bpad, 0.0), writes=["Bpad"])
                bdiag = raw(Bpad, [[128 + 16, 8], [1, 16]])
                bbq = raw(BB, [[16, 8], [1, 16]], extra_off=g0 * 16)
                P.op("pool", lambda h, bdiag=bdiag, bbq=bbq: h.tensor_copy(out=bdiag, in_=bbq), reads=["BB", "Bpad"], writes=["Bpad"])
                for tq4 in range(4):
                    bt = nb()

                    def kmm(h, bt=bt, tq4=tq4, q=q):
                        r = None
                        psv = PS[bt][:, :].rearrange("p (t g c) -> p t g c", t=4, g=8)
                        for gl in range(8):
                            r = h.matmul(psv[:, :, gl, :], lhsT=Bpad3[:, gl, :], rhs=CAqball[q].rearrange("p (g t c) -> p g t c", g=8, t=17)[:, gl, tq4 * 4:(tq4 + 1) * 4, :],
                                         start=True, stop=True)
                        return r
                    P.op("pe", kmm, reads=["Bpad", "CAqb%d" % q], writes=["ps%d" % bt])
                    if tq4 == 0:
                        P.op("dve", lambda h, bt=bt, q=q: h.tensor_tensor(out=Kb4[:, q, 0, :], in0=PS[bt][:, 0:128], in1=Ddiag3[:, q, :], op=ALU.add),
                             reads=["ps%d" % bt, "Ddiag"], writes=["Kb"])
                        P.op("act", lambda h, bt=bt, q=q: h.activation(out=Kb4[:, q, 1:4, :], in_=PS[bt][:, 128:512].rearrange("p (t m) -> p t m", t=3), func=AF.Copy),
                             reads=["ps%d" % bt], writes=["Kb"])
                    else:
                        P.op("act", lambda h, bt=bt, q=q, tq4=tq4: h.activation(
                            out=Kb4[:, q, tq4 * 4:(tq4 + 1) * 4, :], in_=PS[bt][:, :].rearrange("p (t m) -> p t m", t=4), func=AF.Copy),
                            reads=["ps%d" % bt], writes=["Kb"])
            def c2_chain(q):
                g0 = q * 8
                for gp2 in range(4):
                    bt = nb()

                    def ymm(h, bt=bt, gp2=gp2, g0=g0, q=q):
                        r = None
                        for k in range(2):
                            gl = gp2 * 2 + k
                            r = h.matmul(PS[bt][:, k * 256:(k + 1) * 256], lhsT=Hb3[:, g0 + gl, 0:128],
                                         rhs=CAqball[q].rearrange("p (g t c) -> p g t c", g=8, t=17)[:, gl, 1:17, :], start=True, stop=True)
                        return r
                    P.op("pe", ymm, reads=["Hb%d" % g0, "Hb%d" % (g0 + 4), "CAqb%d" % q], writes=["ps%d" % bt])
                    ga = g0 + gp2 * 2
                    yo = raw(Yi, [[16, 2], [128, 16], [1, 16]], extra_off=q * 2048 + (gp2 * 2) * 16)
                    yin = PS[bt][:, :].rearrange("p (k j c) -> p k j c", k=2, j=16)
                    P.op("act", lambda h, yo=yo, yin=yin: h.activation(out=yo, in_=yin, func=AF.Copy),
                         reads=["ps%d" % bt], writes=["Yi"])

            def sample_p1():
                bt0 = nb(); bt1 = nb()
                for (btx, srcH, nm) in [(bt0, H0n3, "H0n"), (bt1, H0n23, "H0n2")]:
                    def trh(h, btx=btx, srcH=srcH):
                        r = None
                        for q in range(4):
                            r = h.transpose(out=PS[btx][:, q * 128:(q + 1) * 128], in_=srcH[:, q, :], identity=ID)
                        return r
                    P.op("pe", trh, reads=[nm, "ID"], writes=["ps%d" % btx])
                P.op("dve", lambda h: h.tensor_scalar(out=AIs, in0=PI3[:, :, 1], scalar1=SGN[:, 0:1], scalar2=None, op0=ALU.mult),
                     reads=["PI", "SGN"], writes=["AIs"])
                ar_b = raw(PR, [[17, G], [0, NS]], extra_off=1)
                ais_b = raw(AIs, [[1, G], [0, NS]])
                ps0v = PS[bt0][:, :].rearrange("p (g b) -> p g b", g=G)
                ps1v = PS[bt1][:, :].rearrange("p (g b) -> p g b", g=G)
                Hn2_3 = Hn2.rearrange("p (g b) -> p g b", g=G)
                P.op("dve", lambda h: h.tensor_tensor(out=Hn3, in0=ps0v, in1=ar_b, op=ALU.mult), reads=["ps%d" % bt0, "PR"], writes=["Hn"])
                P.op("dve", lambda h: h.tensor_tensor(out=Hn2_3, in0=ps1v, in1=ais_b, op=ALU.mult), reads=["ps%d" % bt1, "AIs"], writes=["Hn2"])
                P.op("dve", lambda h: h.tensor_tensor(out=Hn, in0=Hn, in1=Hn2, op=ALU.add), reads=["Hn", "Hn2"], writes=["Hn"])
                P.op("dve", lambda h: h.tensor_tensor(out=Hn, in0=Hn, in1=Xs_sb, op=ALU.add), reads=["Hn", "Xs"], writes=["Hn"])
                P.op("act", lambda h: h.activation(out=Hnb, in_=Hn, func=AF.Copy), reads=["Hn"], writes=["Hnb"])
                btA = nb()

                def tro(h, btA=btA):
                    r = None
                    for q in range(4):
                        r = h.transpose(out=PS[btA][:, q * 128:(q + 1) * 128], in_=Hn[:, q * 128:(q + 1) * 128], identity=ID)
                    return r
                P.op("pe", tro, reads=["Hn", "ID"], writes=["ps%d" % btA])
                P.op("act", lambda h, btA=btA: h.activation(out=Hout, in_=PS[btA][:, :], func=AF.Copy), reads=["ps%d" % btA], writes=["Hout"])
                for gl in range(8):
                    sre_ov = sre_o.rearrange("b (q gl) n -> gl b q n", q=4)[gl]
                    sim_ov = sim_o.rearrange("b (q gl) n -> gl b q n", q=4)[gl]
                    P.dma(sre_ov, Hout3[gl * NS:(gl + 1) * NS, :, 0:64], reads=["Hout"], key="outs")
                    P.dma(sim_ov, Hout3[gl * NS:(gl + 1) * NS, :, 64:128], reads=["Hout"], key="outs")

            aux = P.record(c1b_chain, 0) + P.record(c1b_chain, 1)
            P.replay([P.record(hb_chain, 0, 0), P.record(hb_chain, 1, 0), aux])
            aux = P.record(c2_chain, 0) + P.record(c1b_chain, 2) + P.record(c2_chain, 1) + P.record(c1b_chain, 3) + P.record(sample_p1)
            P.replay([P.record(hb_chain, 2, 0), P.record(hb_chain, 3, 0), aux])
            c2_chain(2)
            c2_chain(3)

            bt = nb()
            P.op("pe", lambda h, bt=bt: h.transpose(out=PS[bt][0:G, 0:128], in_=Hfin, identity=ID), reads=["Hfin%d" % (4 * i_) for i_ in range(8)] + ["ID"], writes=["ps%d" % bt])
            P.op("act", lambda h, bt=bt: h.activation(out=HfT[0:G, :], in_=PS[bt][0:G, 0:128], func=AF.Copy), reads=["ps%d" % bt], writes=["HfT"])
            P.dma(pre, HfT[0:G, 0:64], reads=["HfT"], key="outs")
            P.dma(pim, HfT[0:G, 64:128], reads=["HfT"], key="outs")

            bt = nb()

            def ysm(h, bt=bt):
                r = None
                for q in range(4):
                    h.matmul(PS[bt][:, q * NS:(q + 1) * NS], lhsT=Ddb3[:, q, :], rhs=uT3[:, q, SEQ:SEQ + NS], start=True, stop=False)
                    for gl in range(8):
                        g = q * 8 + gl
                        rr = gl // 2
                        r = h.matmul(PS[bt][32 * rr:32 * rr + 32, q * NS:(q + 1) * NS], lhsT=CAp03[:, g, :], rhs=Hnb3[:, g, :],
                                     start=False, stop=(gl == 7), tile_position=(0, 32 * rr))
                return r
            P.op("pe", ysm, reads=["Ddb", "uT", "CAp0", "Hnb"], writes=["ps%d" % bt])
            P.op("act", lambda h, bt=bt: h.activation(out=SYs, in_=PS[bt][:, 0:4 * NS], func=AF.Gelu_apprx_tanh), reads=["ps%d" % bt], writes=["SYs"])
            P.barrier()
            P.emit(blk)
        if KSTOP == 3:
            return nc

        with ExitStack() as s4:
            blk = s4.enter_context(nc.Block())
            Wout = b16([128, 8 * 1024], 4, 4); Wout3 = Wout.rearrange("p (k f) -> p k f", k=8)
            Wglu = b16([128, 4 * 512], 4, 4); Wglu3 = Wglu.rearrange("p (k f) -> p k f", k=4)
            Gam = f32([128, 1024], 4, 4); Bet = f32([128, 1024], 4, 4)
            STG4 = [f32([128, 1024], 4, 4) for _ in range(2)]
            Xt4 = [f32([128, 1024], 4, 4) for _ in range(2)]
            Hs4 = [f32([128, 1024], 4, 4) for _ in range(4)]
            Hs4n = ["Hs0", "Hs1", "Hs2", "Hs3"]
            SYd = [b16([128, 4 * 512], 4, 4).rearrange("p (q t) -> p q t", q=4),
                   A32.alloc(1024, 4, 4).bitcast(BF16).rearrange("p (q t) -> p q t", q=4)]
            MXd = [b16([128, 4 * 512], 4, 4).rearrange("p (q t) -> p q t", q=4),
                   A32.alloc(1024, 4, 4).bitcast(BF16).rearrange("p (q t) -> p q t", q=4)]
            SGl = [b16([128, 4 * 512], 4, 4).rearrange("p (q t) -> p q t", q=4) for _ in range(2)]
            PMl = [b16([128, 4 * 512], 4, 4).rearrange("p (q t) -> p q t", q=4) for _ in range(2)]
            SIG = [f32([128, 512], 4, 4) for _ in range(2)]
            T1b = [f32([128, 512], 4, 4) for _ in range(2)]
            STATSd = [f32([128, 12], 4, 4) for _ in range(2)]; MVd = [f32([128, 2], 4, 4) for _ in range(2)]
            RSTDd = [f32([128, 1], 4, 4) for _ in range(2)]

            for kc in range(4):
                P.dma(Wglu3[:, kc, :], glu_w[kc * 128:(kc + 1) * 128, :], writes=["Wglu"], key="wglu", queue="pool")
            for kc in range(8):
                P.dma(Wout3[:, kc, :], w_out[kc * 128:(kc + 1) * 128, :], reads=["Wglu"], writes=["Wout"], key="wout", queue="pool")
            P.dma(Gam, ln_g.partition_broadcast(128), writes=["Gam"], key="gb")
            P.dma(Bet, ln_b.partition_broadcast(128), writes=["Bet"], key="gb")

            uTc = uT3[:, :, 0:SEQ].rearrange("p q (c i) -> p q c i", i=TCH)
            tiles = [(2048, NS), (0, 512), (512, 512), (1024, 512), (1536, 512)]
            sub_i = [0]
            def front(ti, t0, nt):
                is_s = (nt == NS)
                c0 = t0 // TCH
                ncl = nt // TCH
                par = ti % 2
                SY3 = SYd[par]; MX3 = MXd[par]
                syn = ["SY%d_%d" % (par, q) for q in range(4)]
                mxn = "MX%d" % par
                if not is_s:
                    for q in range(4):
                        bt = nb()

                        def ymm4(h, bt=bt, q=q, c0=c0, ncl=ncl):
                            r = None
                            psv = PS[bt][:, :].rearrange("p (c j) -> p c j", j=TCH)
                            for tau in range(TCH):
                                r = h.matmul(psv[:, :, tau:TCH], lhsT=Kb4[:, q, tau, :], rhs=uTc[:, q, c0:c0 + ncl, 0:TCH - tau],
                                             start=(tau == 0), stop=False)
                            for j in range(TCH):
                                lw = Yi[:, q * 2048 + j * 128: q * 2048 + (j + 1) * 128]
                                r = h.matmul(psv[:, :, j], lhsT=lw, rhs=IDb[:, c0:c0 + ncl], start=False, stop=(j == TCH - 1))
                            return r
                        P.op("pe", ymm4, reads=["Kb", "uT", "Yi", "IDb"], writes=["ps%d" % bt])
                        P.op("act", lambda h, bt=bt, q=q: h.activation(out=SY3[:, q, 0:nt], in_=PS[bt][:, 0:nt], func=AF.Gelu_apprx_tanh),
                             reads=["ps%d" % bt], writes=[syn[q]])
                else:
                    P.op("pool", lambda h: h.tensor_copy(out=SY3[:, :, 0:NS], in_=SYs3), reads=["SYs"] + syn, writes=syn)
                for q2 in range(4):
                    bt = nb()
                    sl = q2 % 2

                    def gmm(h, bt=bt, q2=q2):
                        r = None
                        for kc in range(4):
                            r = h.matmul(PS[bt][:, 0:nt], lhsT=Wglu3[:, kc, q2 * 128:(q2 + 1) * 128], rhs=SY3[:, kc, 0:nt],
                                         start=(kc == 0), stop=(kc == 3))
                        return r
                    P.op("pe", gmm, reads=["Wglu"] + syn, writes=["ps%d" % bt])
                    P.op("act", lambda h, bt=bt, q2=q2, sl=sl: h.activation(out=SIG[sl][:, 0:nt], in_=PS[bt][:, 0:nt], func=AF.Sigmoid,
                                                                            bias=GB[:, q2:q2 + 1]),
                         reads=["ps%d" % bt, "GB"], writes=["SIG%d" % sl])
                    P.op("dve", lambda h, q2=q2, sl=sl: h.tensor_tensor(out=T1b[sl][:, 0:nt], in0=SIG[sl][:, 0:nt], in1=SY3[:, q2, 0:nt], op=ALU.mult),
                         reads=["SIG%d" % sl, syn[q2]], writes=["T1b%d" % sl])
                    P.op("dve", lambda h, q2=q2, sl=sl: h.tensor_tensor(out=MX3[:, q2, 0:nt], in0=T1b[sl][:, 0:nt], in1=SGl[par][:, q2, 0:nt], op=ALU.mult),
                         reads=["T1b%d" % sl, "SGl%d" % par], writes=[mxn])

            def back(ti, t0, nt):
                is_s = (nt == NS)
                par = ti % 2
                MX3 = MXd[par]
                mxn = "MX%d" % par
                nsub = 1 if is_s else 4
                def one_sub(sidx):
                    ns = NS if is_s else 128
                    sl = sub_i[0] % 2
                    hs = sub_i[0] % 4
                    HsT = Hs4[hs]; hsn = Hs4n[hs]
                    sub_i[0] += 1
                    STATS = STATSd[sl]; MV = MVd[sl]; RSTD = RSTDd[sl]
                    for half in range(2):
                        bt = nb()

                        def omm(h, bt=bt, half=half, sidx=sidx, ns=ns):
                            r = None
                            for kc in range(8):
                                if kc < 4:
                                    lw = MX3[:, kc, sidx * 128: sidx * 128 + ns]
                                else:
                                    lw = PMl[par][:, kc - 4, sidx * 128: sidx * 128 + ns]
                                r = h.matmul(PS[bt][0:ns, :], lhsT=lw, rhs=Wout3[:, kc, half * 512:(half + 1) * 512],
                                             start=(kc == 0), stop=(kc == 7))
                            return r
                        P.op("pe", omm, reads=[mxn, "PMl%d" % par, "Wout"], writes=["ps%d" % bt])
                        P.op("dve", lambda h, bt=bt, half=half, sl=sl, ns=ns: h.scalar_tensor_tensor(
                            out=HsT[0:ns, half * 512:(half + 1) * 512], in0=Xt4[sl][0:ns, half * 512:(half + 1) * 512], scalar=DN_ALPHA,
                            in1=PS[bt][0:ns, :], op0=ALU.mult, op1=ALU.add), reads=["ps%d" % bt, "Xt4%d" % sl], writes=[hsn])
                        P.op("dve", lambda h, half=half, sl=sl, ns=ns, STATS=STATS: h.bn_stats(out=STATS[0:ns, half * 6:(half + 1) * 6],
                                                                               in_=HsT[0:ns, half * 512:(half + 1) * 512]),
                             reads=[hsn], writes=["STATS%d" % sl])
                    load_x(sub_i[0] + 1)
                    P.op("dve", lambda h, ns=ns, STATS=STATS, MV=MV: h.bn_aggr(out=MV[0:ns, :], in_=STATS[0:ns, :]), reads=["STATS%d" % sl], writes=["MV%d" % sl])
                    P.op("act", lambda h, ns=ns, MV=MV, RSTD=RSTD: h.activation(out=RSTD[0:ns, :], in_=MV[0:ns, 1:2], func=AF.Sqrt, bias=EPS[0:ns, :]),
                         reads=["MV%d" % sl, "EPS"], writes=["RSTD%d" % sl])
                    P.op("dve", lambda h, ns=ns, RSTD=RSTD: h.reciprocal(out=RSTD[0:ns, :], in_=RSTD[0:ns, :]), reads=["RSTD%d" % sl], writes=["RSTD%d" % sl])
                    P.op("dve", lambda h, sl=sl, ns=ns, MV=MV, RSTD=RSTD: h.tensor_scalar(out=HsT[0:ns, :], in0=HsT[0:ns, :], scalar1=MV[0:ns, 0:1],
                                                                      scalar2=RSTD[0:ns, 0:1], op0=ALU.subtract, op1=ALU.mult),
                         reads=[hsn, "MV%d" % sl, "RSTD%d" % sl], writes=[hsn])
                    P.op("pool", lambda h, sl=sl, ns=ns: h.tensor_tensor(out=HsT[0:ns, :], in0=HsT[0:ns, :], in1=Gam[0:ns, :], op=ALU.mult),
                         reads=[hsn, "Gam"], writes=[hsn])
                    P.op("pool", lambda h, sl=sl, ns=ns: h.tensor_tensor(out=HsT[0:ns, :], in0=HsT[0:ns, :], in1=Bet[0:ns, :], op=ALU.add),
                         reads=[hsn, "Bet"], writes=[hsn])
                    dst = ys if is_s else yp[t0 + sidx * 128: t0 + (sidx + 1) * 128, :]
                    P.dma(dst, HsT[0:ns, :], reads=[hsn], key="outy%d" % hs)
                for sidx_ in range(nsub):
                    one_sub(sidx_)

            def load_wout():
                pass

            all_subs = []
            for (t0_, nt_) in tiles:
                if nt_ == NS:
                    all_subs.append((xs, NS))
                else:
                    for sidx_ in range(4):
                        all_subs.append((xp[t0_ + sidx_ * 128: t0_ + (sidx_ + 1) * 128, :], 128))

            def load_x(k):
                if k >= len(all_subs):
                    return
                src_, ns_ = all_subs[k]
                sl_ = k % 2
                P.dma(Xt4[sl_][0:ns_], src_, writes=["Xt4%d" % sl_], key="xt4%d" % sl_)

            def load_sgpm(ti):
                if ti >= len(tiles):
                    return
                t0_, nt_ = tiles[ti]
                P.dma(SGl[ti % 2][:, :, 0:nt_], SGscr[:, :, t0_:t0_ + nt_], writes=["SGl%d" % (ti % 2)], key="sgl%d" % (ti % 2), queue="act")
                P.dma(PMl[ti % 2][:, :, 0:nt_], PMscr[:, :, t0_:t0_ + nt_], writes=["PMl%d" % (ti % 2)], key="pml%d" % (ti % 2), queue="act")

            load_sgpm(0)
            load_sgpm(1)
            front(0, *tiles[0])
            load_wout()
            load_x(0)
            load_x(1)
            for ti_ in range(1, len(tiles)):
                front(ti_, *tiles[ti_])
                back(ti_ - 1, *tiles[ti_ - 1])
                load_sgpm(ti_ + 1)
            back(len(tiles) - 1, *tiles[-1])
            P.barrier()
            P.emit(blk)
    return nc


_PROG = None


def kernel(**inputs):
    global _PROG
    f = lambda a: np.ascontiguousarray(np.asarray(a, dtype=np.float32))
    xpf = f(inputs["x_prompt"]); xsf = f(inputs["x_sample"])
    sref = f(inputs["state_ssm_re"])[0]; simf = f(inputs["state_ssm_im"])[0]; spf = f(inputs["state_pool"])[0]
    shared = {
        "w_in": f(inputs["w_in"])[0], "lam_re": f(inputs["ssm_lambda_re"])[0], "lam_im": f(inputs["ssm_lambda_im"])[0],
        "log_dt": f(inputs["ssm_log_dt"]).reshape(1, G), "b_re": f(inputs["ssm_b_re"])[0], "b_im": f(inputs["ssm_b_im"])[0],
        "c_re": f(inputs["ssm_c_re"])[0], "c_im": f(inputs["ssm_c_im"])[0], "ssm_d": f(inputs["ssm_d"])[0],
        "glu_w": f(inputs["glu_w"])[0], "glu_b": f(inputs["glu_b"])[0], "pool_w": f(inputs["pool_w"])[0],
        "pool_scale": f(inputs["pool_scale"])[0], "w_out": f(inputs["w_out"])[0],
        "ln_g": f(inputs["ln_g"]).reshape(1, 1024), "ln_b": f(inputs["ln_b"]).reshape(1, 1024),
    }
    in_maps = []
    for c in range(8):
        m = dict(shared)
        m["xp"] = np.ascontiguousarray(xpf[c])
        m["xs"] = np.ascontiguousarray(xsf[c * NS:(c + 1) * NS, 0, :])
        m["sre"] = np.ascontiguousarray(sref[c * NS:(c + 1) * NS])
        m["sim"] = np.ascontiguousarray(simf[c * NS:(c + 1) * NS])
        m["spool"] = np.ascontiguousarray(spf[c * NS:(c + 1) * NS])
        in_maps.append(m)
    if _PROG is None:
        _PROG = build_program()
    res = run_bass_kernel_spmd(_PROG, in_maps, core_ids=list(range(8)))
    R = res.results
    y_prompt = np.stack([R[c]["yp"] for c in range(8)], 0).astype(np.float32)
    y_sample = np.concatenate([R[c]["ys"] for c in range(8)], 0).reshape(128, 1, D_MODEL).astype(np.float32)
    pre = np.stack([R[c]["pre"] for c in range(8)], 0)[None].astype(np.float32)
    pim = np.stack([R[c]["pim"] for c in range(8)], 0)[None].astype(np.float32)
    ppool = np.stack([R[c]["ppool"] for c in range(8)], 0)[None].astype(np.float32)
    sre_o = np.concatenate([R[c]["sre_o"] for c in range(8)], 0)[None].astype(np.float32)
    sim_o = np.concatenate([R[c]["sim_o"] for c in range(8)], 0)[None].astype(np.float32)
    spool_o = np.concatenate([R[c]["spool_o"] for c in range(8)], 0)[None].astype(np.float32)
    return (y_prompt, y_sample, pre, pim, ppool, sre_o, sim_o, spool_o)
```

```python
import math
import numpy as np
import concourse.bass as bass
import concourse.mybir as mybir
from concourse.bass_utils import run_bass_kernel_spmd
from contextlib import ExitStack

F32 = mybir.dt.float32
BF16 = mybir.dt.bfloat16
I32 = mybir.dt.int32
AF = mybir.ActivationFunctionType
ALU = mybir.AluOpType
AX = mybir.AxisListType

ENGS = ["pe", "act", "dve", "pool", "sp"]

D_MODEL = 1024
SEQ = 2048
NS = 16
NTOK = SEQ + NS
G = 32
TCH = 16
NCH = SEQ // TCH
POOL_WINDOWS = (2, 4, 8, 16)
DN_ALPHA = 2.0 ** 0.25
LN_EPS = 1e-5
TWO_PI = 2.0 * math.pi
MAGIC = 12582912.0
import os
KSTOP = int(os.environ.get('KSTOP', '0'))
KSUB = int(os.environ.get('KSUB', '0'))


class Prog:
    def __init__(self, nc, stack):
        self.nc = nc
        self.stack = stack
        self.q = {e: [] for e in ENGS}
        self.esem = {e: stack.enter_context(nc.semaphore("se_" + e)) for e in ENGS}
        self.cnt = {e: 0 for e in ENGS}
        self.waited = {e: {} for e in ENGS}
        self.res_w = {}
        self.res_r = {}
        self.dsem = {}
        self.dcnt = {}
        self.rec = None

    def _semof(self, key):
        if key[0] == "eng":
            return self.esem[key[1]]
        return self.dsem[key[1]]

    def _deps(self, reads, writes):
        deps = []
        for r in reads:
            t = self.res_w.get(r)
            if t is not None:
                deps.append(t)
        for w in writes:
            t = self.res_w.get(w)
            if t is not None:
                deps.append(t)
            deps.extend(self.res_r.get(w, []))
        return deps

    def _emit_waits(self, eng, deps):
        need = {}
        for (k, v) in deps:
            if k[0] == "dma":
                v = max(v, 16 * self.dcnt[k[1]])
            if k == ("eng", "pe") and eng == "pe":
                continue
            if v > need.get(k, 0):
                need[k] = v
        for k, v in need.items():
            if self.waited[eng].get(k, 0) >= v:
                continue
            self.waited[eng][k] = v
            sem = self._semof(k)
            self.q[eng].append(lambda h, sem=sem, v=v: h.wait_ge(sem, v))

    def _update(self, tok, reads, writes):
        for w in writes:
            self.res_w[w] = tok
            self.res_r[w] = []
        for r in reads:
            if r in writes:
                continue
            self.res_r.setdefault(r, []).append(tok)

    def record(self, f, *a, **k):
        assert self.rec is None
        self.rec = []
        f(*a, **k)
        r = self.rec
        self.rec = None
        return r

    def replay(self, chains):
        seen = {}
        for ci, ch in enumerate(chains):
            for kind, a, k in ch:
                for w in k["writes"]:
                    if w.startswith("ps"):
                        assert seen.setdefault(w, ci) == ci, ("PSUM bank shared between interleaved chains", w)
        idx = [0] * len(chains)
        alive = True
        while alive:
            alive = False
            for ci, ch in enumerate(chains):
                if idx[ci] < len(ch):
                    kind, a, k = ch[idx[ci]]
                    idx[ci] += 1
                    alive = True
                    if kind == "op":
                        self.op(*a, **k)
                    else:
                        self.dma(*a, **k)

    def op(self, eng, fn, reads=(), writes=()):
        if self.rec is not None:
            self.rec.append(("op", (eng, fn), dict(reads=list(reads), writes=list(writes))))
            return None
        reads = list(reads)
        writes = list(writes)
        self._emit_waits(eng, self._deps(reads, writes))
        sem = self.esem[eng]
        self.q[eng].append(lambda h, fn=fn, sem=sem: fn(h).then_inc(sem, 1))
        self.cnt[eng] += 1
        tok = (("eng", eng), self.cnt[eng])
        self._update(tok, reads, writes)
        return tok

    def dma(self, out, in_, reads=(), writes=(), key="d", queue="sp", **kw):
        if self.rec is not None:
            k = dict(reads=list(reads), writes=list(writes), key=key, queue=queue)
            k.update(kw)
            self.rec.append(("dma", (out, in_), k))
            return None
        reads = list(reads)
        writes = list(writes)
        if key not in self.dsem:
            self.dsem[key] = self.stack.enter_context(self.nc.semaphore("sd_" + key))
            self.dcnt[key] = 0
        self._emit_waits(queue, self._deps(reads, writes))
        sem = self.dsem[key]
        self.q[queue].append(
            lambda h, out=out, in_=in_, sem=sem, kw=kw: h.dma_start(out=out, in_=in_, **kw).then_inc(sem, 16)
        )
        self.dcnt[key] += 1
        tok = (("dma", key), 16 * self.dcnt[key])
        self._update(tok, reads, writes)
        return tok

    def barrier(self):
        for e in ENGS:
            for key in self.dsem:
                v = 16 * self.dcnt[key]
                k = ("dma", key)
                if v > 0 and self.waited[e].get(k, 0) < v:
                    self.waited[e][k] = v
                    sem = self.dsem[key]
                    self.q[e].append(lambda h, sem=sem, v=v: h.wait_ge(sem, v))
            for e2 in ["pe", "act", "dve", "pool"]:
                v = self.cnt[e2]
                k = ("eng", e2)
                if e2 != e and v > 0 and self.waited[e].get(k, 0) < v:
                    self.waited[e][k] = v
                    sem = self.esem[e2]
                    self.q[e].append(lambda h, sem=sem, v=v: h.wait_ge(sem, v))
        self.res_w = {}
        self.res_r = {}

    def emit(self, block):
        qs = self.q
        self.q = {e: [] for e in ENGS}

        @block.sync
        def _(h):
            for f in qs["sp"]:
                f(h)

        @block.tensor
        def _(h):
            for f in qs["pe"]:
                f(h)

        @block.scalar
        def _(h):
            for f in qs["act"]:
                f(h)

        @block.vector
        def _(h):
            for f in qs["dve"]:
                f(h)

        @block.gpsimd
        def _(h):
            for f in qs["pool"]:
                f(h)


class Arena:
    def __init__(self, tensor, n):
        self.t = tensor
        self.n = n
        self.items = []

    def alloc(self, size, b0, b1):
        osize = size
        size = (size + 7) // 8 * 8
        off = 0
        while True:
            conf = [it for it in self.items
                    if not (it[3] < b0 or it[2] > b1) and not (it[0] + it[1] <= off or off + size <= it[0])]
            if not conf:
                break
            off = max(it[0] + it[1] for it in conf)
        assert off + size <= self.n, ("arena overflow", off, size, self.n)
        self.items.append((off, size, b0, b1))
        return self.t[:, off:off + osize]


def raw(view, free, extra_off=0):
    return bass.AP(view.tensor, view.offset + extra_off, [list(view.ap[0])] + [list(f) for f in free])


N32 = 24700
N16 = 52000


def build_program():
    nc = bass.Bass("TRN2", target_bir_lowering=False)

    def din(name, shape):
        return nc.dram_tensor(name, list(shape), F32, kind="ExternalInput").ap()

    def dout(name, shape):
        return nc.dram_tensor(name, list(shape), F32, kind="ExternalOutput").ap()

    xp = din("xp", [SEQ, D_MODEL]); xs = din("xs", [NS, D_MODEL])
    sre = din("sre", [NS, G, 64]); sim = din("sim", [NS, G, 64]); spool = din("spool", [NS, 15, 512])
    w_in = din("w_in", [1024, 2048]); lam_re = din("lam_re", [G, 64]); lam_im = din("lam_im", [G, 64])
    log_dt = din("log_dt", [1, G]); b_re = din("b_re", [G, 64, 16]); b_im = din("b_im", [G, 64, 16])
    c_re = din("c_re", [G, 16, 64]); c_im = din("c_im", [G, 16, 64]); ssm_d = din("ssm_d", [512])
    glu_w = din("glu_w", [512, 512]); glu_b = din("glu_b", [512]); pool_w = din("pool_w", [4, 128, 128])
    pool_scale = din("pool_scale", [512]); w_out = din("w_out", [1024, 1024])
    ln_g = din("ln_g", [1, 1024]); ln_b = din("ln_b", [1, 1024])

    yp = dout("yp", [SEQ, D_MODEL]); ys = dout("ys", [NS, D_MODEL])
    pre = dout("pre", [G, 64]); pim = dout("pim", [G, 64]); ppool = dout("ppool", [15, 512])
    sre_o = dout("sre_o", [NS, G, 64]); sim_o = dout("sim_o", [NS, G, 64]); spool_o = dout("spool_o", [NS, 15, 512])

    with ExitStack() as st:
        A32t = st.enter_context(nc.sbuf_tensor("A32", [128, N32], F32))
        A16t = st.enter_context(nc.sbuf_tensor("A16", [128, N16], BF16))
        PS = [st.enter_context(nc.psum_tensor("ps%d" % i, [128, 512], F32)) for i in range(8)]
        A32 = Arena(A32t, N32)
        A16 = Arena(A16t, N16)
        P = Prog(nc, st)
        bank_ctr = [0]

        def nb():
            b = bank_ctr[0] % 8
            bank_ctr[0] += 1
            return b

        def f32(shape, b0, b1):
            n = int(np.prod(shape[1:]))
            v = A32.alloc(n, b0, b1)
            return v[0:shape[0]] if shape[0] < 128 else v

        def b16(shape, b0, b1):
            n = int(np.prod(shape[1:]))
            v = A16.alloc(n, b0, b1)
            return v[0:shape[0]] if shape[0] < 128 else v

        ID = f32([128, 128], 1, 4)
        IDb = b16([128, 128], 1, 4)
        uT = b16([128, 4 * NTOK], 1, 4)
        SGscr = nc.dram_tensor("sg_scr", [128, 4, NTOK], BF16).ap()
        PMscr = nc.dram_tensor("pm_scr", [128, 4, NTOK], BF16).ap()
        uT3 = uT.rearrange("p (q t) -> p q t", q=4)
        HALFPI = f32([128, 1], 1, 4)
        EPS = f32([128, 1], 1, 4)
        SGN = f32([128, 1], 1, 4)
        NSGN = f32([128, 1], 1, 4)
        M0 = f32([128, 1], 1, 4)
        M1 = f32([128, 1], 1, 4)
        Dv = f32([128, 4], 1, 4)
        PSC = f32([128, 4], 1, 4)
        GB = f32([128, 4], 1, 4)
        IOT = st.enter_context(nc.sbuf_tensor("IOT", [128, 1], I32))
        IOT2 = st.enter_context(nc.sbuf_tensor("IOT2", [128, 1], I32))
        IOTF = st.enter_context(nc.sbuf_tensor("IOTF", [128, 128], I32))
        CIDX = f32([128, 128], 1, 4)
        L2 = f32([128, 256], 1, 2)
        Dt = f32([128, G], 1, 2)
        Bcat = f32([128, G * 32], 1, 2)
        Bcat3 = Bcat.rearrange("p (g x) -> p g x", g=G)
        S_sb = f32([128, G * NCH], 2, 3)
        S3 = S_sb.rearrange("p (g c) -> p g c", g=G)
        PR = f32([128, G * 17], 1, 3); PI = f32([128, G * 17], 1, 3); PIs = f32([128, G * 17], 1, 3)
        PR3 = PR.rearrange("p (g t) -> p g t", g=G); PI3 = PI.rearrange("p (g t) -> p g t", g=G)
        PIs3 = PIs.rearrange("p (g t) -> p g t", g=G)
        BB = f32([128, G * 16], 1, 3); BBsw = f32([128, G * 16], 1, 3)
        THr = f32([128, G], 1, 3); LR = f32([128, G], 1, 3)
        Xs_sb = f32([128, G * NS], 2, 3)
        Kb = b16([128, 4 * 16 * 128], 3, 4)
        Kb4 = Kb.rearrange("p (q t m) -> p q t m", q=4, t=16)
        Yi = b16([128, G * 256], 3, 4)
        Yi3 = Yi.rearrange("p (g x) -> p g x", g=G)
        SYs = b16([128, 4 * NS], 3, 4)
        SYs3 = SYs.rearrange("p (q t) -> p q t", q=4)

        def reduce_angle(eng, dst, src, tmpb, n_dst, n_src, n_tmp):
            P.op(eng, lambda h: h.tensor_scalar(out=tmpb, in0=src, scalar1=1.0 / TWO_PI, scalar2=MAGIC, op0=ALU.mult, op1=ALU.add),
                 reads=[n_src], writes=[n_tmp])
            P.op(eng, lambda h: h.tensor_scalar(out=tmpb, in0=tmpb, scalar1=-MAGIC, scalar2=None, op0=ALU.add),
                 reads=[n_tmp], writes=[n_tmp])
            P.op(eng, lambda h: h.scalar_tensor_tensor(out=dst, in0=tmpb, scalar=-TWO_PI, in1=src, op0=ALU.mult, op1=ALU.add),
                 reads=[n_tmp, n_src], writes=[n_dst])
            P.op(eng, lambda h: h.tensor_scalar(out=dst, in0=dst, scalar1=math.pi, scalar2=-math.pi, op0=ALU.min, op1=ALU.max),
                 reads=[n_dst], writes=[n_dst])

        def sincos(sin_dst, cos_dst, ang, tmpb, n_sin, n_cos, n_ang, n_tmp):
            P.op("act", lambda h: h.activation(out=sin_dst, in_=ang, func=AF.Sin), reads=[n_ang], writes=[n_sin])
            P.op("act", lambda h: h.activation(out=tmpb, in_=ang, func=AF.Abs),
                 reads=[n_ang], writes=[n_tmp])
            P.op("act", lambda h: h.activation(out=cos_dst, in_=tmpb, func=AF.Sin, bias=HALFPI, scale=-1.0),
                 reads=[n_tmp, "HALFPI"], writes=[n_cos])


        with ExitStack() as s1:
            blk = s1.enter_context(nc.Block())
            Win = b16([128, 8 * 2048], 1, 1)
            Win3 = Win.rearrange("p (k f) -> p k f", k=8)
            Wpool = b16([128, 4 * 128], 1, 1)
            Wpool3 = Wpool.rearrange("p (g d) -> p g d", g=4)
            STG = [f32([128, 2048], 1, 1) for _ in range(2)]
            Xt = [f32([128, 1024], 1, 1) for _ in range(2)]
            xTd = [b16([128, 8 * 512], 1, 1).rearrange("p (k t) -> p k t", k=8) for _ in range(2)]
            SGt = [b16([128, 4 * 512], 1, 1).rearrange("p (q t) -> p q t", q=4) for _ in range(2)]
            PMt = [b16([128, 4 * 512], 1, 1).rearrange("p (q t) -> p q t", q=4) for _ in range(2)]
            PinT = [b16([128, 512], 1, 1) for _ in range(2)]
            V = [f32([128, 15 + 512], 1, 1) for _ in range(4)]
            W1 = f32([128, 15 + 512], 1, 1); W2 = f32([128, 15 + 512], 1, 1)
            SPG = [f32([128, 512], 1, 1) for _ in range(2)]
            TMP = [f32([128, 512], 1, 1) for _ in range(2)]
            INVC = f32([128, 16], 1, 1)
            PLAST = f32([128, 4 * 16], 1, 1)
            PLAST3 = PLAST.rearrange("p (g t) -> p g t", g=4)
            PinS = f32([128, 4 * 16], 1, 1)
            PinS3 = PinS.rearrange("p (g t) -> p g t", g=4)
            SPn = f32([128, 2 * 512], 1, 1)
            SPn3 = SPn.rearrange("p (h c) -> p h c", h=2)
            PfT = f32([128, 4 * 240], 1, 1)
            PfT3 = PfT.rearrange("p (g r) -> p g r", g=4)
            WSUM = f32([128, 16], 1, 1)
            PoS = b16([128, 16], 1, 1)
            ROW = f32([128, 512], 1, 1)

            LamR = f32([128, G], 1, 1); LamI = f32([128, G], 1, 1)
            TH = f32([128, G], 1, 1)
            TAU = f32([128, 17], 1, 1)
            A1 = f32([128, G * 17], 1, 1); A2 = f32([128, G * 17], 1, 1); A3 = f32([128, G * 17], 1, 1)
            t32 = [f32([128, G], 1, 1) for _ in range(8)]
            T1 = f32([128, G * 16], 1, 1); T2 = f32([128, G * 16], 1, 1); T3 = f32([128, G * 16], 1, 1)
            P.op("pool", lambda h: h.memset(W1[:, 0:128], 1.0), writes=["W1"])
            P.op("pool", lambda h: h.affine_select(out=ID, in_=W1[:, 0:128], pattern=[[-1, 128]],
                                                   compare_op=ALU.is_equal, fill=0.0, base=0, channel_multiplier=1),
                 reads=["W1"], writes=["ID"])
            P.op("dve", lambda h: h.tensor_copy(out=IDb, in_=ID), reads=["ID"], writes=["IDb"])
            P.op("dve", lambda h: h.memset(HALFPI, math.pi / 2), writes=["HALFPI"])
            P.op("dve", lambda h: h.memset(EPS, LN_EPS), writes=["EPS"])
            P.op("pool", lambda h: h.iota(IOT[:], [[0, 1]], base=0, channel_multiplier=1), writes=["IOT"])
            P.op("pool", lambda h: h.iota(IOTF[:], [[1, 128]], base=0, channel_multiplier=0), writes=["IOTF"])
            P.op("dve", lambda h: h.tensor_copy(out=CIDX, in_=IOTF[:]), reads=["IOTF"], writes=["CIDX"])
            P.op("dve", lambda h: h.tensor_copy(out=SGN, in_=IOT[:]), reads=["IOT"], writes=["SGN"])
            P.op("dve", lambda h: h.tensor_scalar(out=SGN, in0=SGN, scalar1=63.5, scalar2=None, op0=ALU.is_gt),
                 reads=["SGN"], writes=["SGN"])
            P.op("dve", lambda h: h.tensor_scalar(out=SGN, in0=SGN, scalar1=2.0, scalar2=-1.0, op0=ALU.mult, op1=ALU.add),
                 reads=["SGN"], writes=["SGN"])
            P.op("dve", lambda h: h.tensor_scalar(out=NSGN, in0=SGN, scalar1=-1.0, scalar2=None, op0=ALU.mult),
                 reads=["SGN"], writes=["NSGN"])
            P.op("dve", lambda h: h.tensor_copy(out=M1, in_=IOT[:]), reads=["IOT"], writes=["M1"])
            P.op("dve", lambda h: h.tensor_scalar(out=M1, in0=M1, scalar1=1.0 / 16, scalar2=-0.46875, op0=ALU.mult, op1=ALU.add),
                 reads=["M1"], writes=["M1"])
            P.op("dve", lambda h: h.tensor_scalar(out=M1, in0=M1, scalar1=MAGIC, scalar2=-MAGIC, op0=ALU.add, op1=ALU.add),
                 reads=["M1"], writes=["M1"])
            P.op("dve", lambda h: h.tensor_scalar(out=M0, in0=M1, scalar1=0.5, scalar2=-0.25, op0=ALU.mult, op1=ALU.add),
                 reads=["M1"], writes=["M0"])
            P.op("dve", lambda h: h.tensor_scalar(out=M0, in0=M0, scalar1=MAGIC, scalar2=-MAGIC, op0=ALU.add, op1=ALU.add),
                 reads=["M0"], writes=["M0"])
            P.op("dve", lambda h: h.scalar_tensor_tensor(out=M1, in0=M0, scalar=-2.0, in1=M1, op0=ALU.mult, op1=ALU.add),
                 reads=["M0", "M1"], writes=["M1"])
            P.op("dve", lambda h: h.tensor_scalar(out=M0, in0=M1, scalar1=-1.0, scalar2=1.0, op0=ALU.mult, op1=ALU.add),
                 reads=["M1"], writes=["M0"])
            P.op("dve", lambda h: h.tensor_scalar(out=INVC, in0=CIDX[:, 0:16], scalar1=1.0, scalar2=None, op0=ALU.add),
                 reads=["CIDX"], writes=["INVC"])
            P.op("dve", lambda h: h.reciprocal(out=INVC, in_=INVC), reads=["INVC"], writes=["INVC"])
            def weight_chain():
                for cb in range(4):
                    for kh in range(2):
                        srcw = w_in[kh * 512:(kh + 1) * 512, cb * 512:(cb + 1) * 512].rearrange("(k p) f -> p k f", p=128)
                        dstw = Win3[:, kh * 4:(kh + 1) * 4, cb * 512:(cb + 1) * 512]
                        prev_ = ["Win_%d_0" % (cb - 1), "Win_%d_1" % (cb - 1)] if cb > 0 else []
                        P.dma(dstw, srcw, reads=prev_, writes=["Win_%d_%d" % (cb, kh)], key="win%d" % cb, queue="pool")
                P.dma(Wpool3, pool_w.rearrange("g c d -> c g d"), writes=["Wpool"], key="wpool", queue="pool")

            def param_head():
                bt = nb()

                def trl2(h, bt=bt):
                    h.transpose(out=PS[bt][:, 0:G], in_=L2[0:G, 0:128], identity=ID[0:G, 0:G])
                    return h.transpose(out=PS[bt][:, G:2 * G], in_=L2[0:G, 128:256], identity=ID[0:G, 0:G])
                P.op("pe", trl2, reads=["L2", "ID"], writes=["ps%d" % bt])
                P.op("dve", lambda h, bt=bt: h.tensor_copy(out=LamR, in_=PS[bt][:, 0:G]), reads=["ps%d" % bt], writes=["LamR"])
                P.op("dve", lambda h, bt=bt: h.tensor_copy(out=LamI, in_=PS[bt][:, G:2 * G]), reads=["ps%d" % bt], writes=["LamI"])

            def param_chain():
                P.op("act", lambda h: h.activation(out=Dt, in_=Dt, func=AF.Exp), reads=["Dt"], writes=["Dt"])
                P.op("dve", lambda h: h.tensor_tensor(out=TH, in0=LamI, in1=Dt, op=ALU.mult), reads=["LamI", "Dt"], writes=["TH"])
                P.op("dve", lambda h: h.tensor_tensor(out=LR, in0=LamR, in1=Dt, op=ALU.mult), reads=["LamR", "Dt"], writes=["LR"])
                reduce_angle("dve", THr, TH, t32[0], "THr", "TH", "t0")
                P.op("dve", lambda h: h.tensor_copy(out=TAU, in_=CIDX[:, 0:17]), reads=["CIDX"], writes=["TAU"])
                thr_b = raw(THr, [[1, G], [0, 17]])
                lr_b = raw(LR, [[1, G], [0, 17]])
                tau_b = raw(TAU, [[0, G], [1, 17]])
                A1_3 = A1.rearrange("p (g t) -> p g t", g=G)
                A2_3 = A2.rearrange("p (g t) -> p g t", g=G)
                A3_3 = A3.rearrange("p (g t) -> p g t", g=G)
                P.op("dve", lambda h: h.tensor_tensor(out=A1_3, in0=thr_b, in1=tau_b, op=ALU.mult), reads=["THr", "TAU"], writes=["A1"])
                reduce_angle("dve", A2, A1, A3, "A2", "A1", "A3")
                sincos(PI, PR, A2, A3, "PI", "PR", "A2", "A3")
                P.op("dve", lambda h: h.tensor_tensor(out=A1_3, in0=lr_b, in1=tau_b, op=ALU.mult), reads=["LR", "TAU", "A2"], writes=["A1"])
                P.op("act", lambda h: h.activation(out=A1, in_=A1, func=AF.Exp), reads=["A1"], writes=["A1"])
                P.op("dve", lambda h: h.tensor_tensor(out=PR, in0=PR, in1=A1, op=ALU.mult), reads=["PR", "A1"], writes=["PR"])
                P.op("dve", lambda h: h.tensor_tensor(out=PI, in0=PI, in1=A1, op=ALU.mult), reads=["PI", "A1"], writes=["PI"])
                P.op("dve", lambda h: h.tensor_scalar(out=PIs, in0=PI, scalar1=SGN[:, 0:1], scalar2=None, op0=ALU.mult),
                     reads=["PI", "SGN"], writes=["PIs"])
                ar = PR3[:, :, 1]; ai = PI3[:, :, 1]
                den, rden, nr, qr, qi, tq = t32[1], t32[2], t32[3], t32[4], t32[5], t32[6]
                P.op("dve", lambda h: h.tensor_tensor(out=den, in0=LamR, in1=LamR, op=ALU.mult), reads=["LamR"], writes=["den"])
                P.op("dve", lambda h: h.tensor_tensor(out=tq, in0=LamI, in1=LamI, op=ALU.mult), reads=["LamI"], writes=["tq"])
                P.op("dve", lambda h: h.tensor_tensor(out=den, in0=den, in1=tq, op=ALU.add), reads=["den", "tq"], writes=["den"])
                P.op("dve", lambda h: h.reciprocal(out=rden, in_=den), reads=["den"], writes=["rden"])
                P.op("dve", lambda h: h.tensor_scalar(out=nr, in0=ar, scalar1=-1.0, scalar2=None, op0=ALU.add), reads=["PR"], writes=["nr"])
                P.op("dve", lambda h: h.tensor_tensor(out=qr, in0=nr, in1=LamR, op=ALU.mult), reads=["nr", "LamR"], writes=["qr"])
                P.op("dve", lambda h: h.tensor_tensor(out=tq, in0=ai, in1=LamI, op=ALU.mult), reads=["PI", "LamI", "den"], writes=["tq"])
                P.op("dve", lambda h: h.tensor_tensor(out=qr, in0=qr, in1=tq, op=ALU.add), reads=["qr", "tq"], writes=["qr"])
                P.op("dve", lambda h: h.tensor_tensor(out=qr, in0=qr, in1=rden, op=ALU.mult), reads=["qr", "rden"], writes=["qr"])
                P.op("dve", lambda h: h.tensor_tensor(out=qi, in0=ai, in1=LamR, op=ALU.mult), reads=["PI", "LamR"], writes=["qi"])
                P.op("dve", lambda h: h.tensor_tensor(out=tq, in0=nr, in1=LamI, op=ALU.mult), reads=["nr", "LamI", "qr"], writes=["tq"])
                P.op("dve", lambda h: h.tensor_tensor(out=qi, in0=qi, in1=tq, op=ALU.subtract), reads=["qi", "tq"], writes=["qi"])
                P.op("dve", lambda h: h.tensor_tensor(out=qi, in0=qi, in1=rden, op=ALU.mult), reads=["qi", "rden"], writes=["qi"])
                P.op("act", lambda h: h.activation(out=Bcat[64:128, :], in_=Bcat[0:64, :], func=AF.Copy), reads=["Bcat"], writes=["Bcat"])
                bre = Bcat3[:, :, 0:16]; bim = Bcat3[:, :, 16:32]
                qr_b = raw(qr, [[1, G], [0, 16]]); qi_b = raw(qi, [[1, G], [0, 16]])
                T1_3 = T1.rearrange("p (g c) -> p g c", g=G); T2_3 = T2.rearrange("p (g c) -> p g c", g=G)
                T3_3 = T3.rearrange("p (g c) -> p g c", g=G)
                P.op("dve", lambda h: h.tensor_tensor(out=T1_3, in0=bre, in1=qr_b, op=ALU.mult), reads=["Bcat", "qr"], writes=["T1"])
                P.op("dve", lambda h: h.tensor_tensor(out=T3_3, in0=bim, in1=qi_b, op=ALU.mult), reads=["Bcat", "qi"], writes=["T3"])
                P.op("dve", lambda h: h.tensor_tensor(out=T1, in0=T1, in1=T3, op=ALU.subtract), reads=["T1", "T3"], writes=["T1"])
                P.op("dve", lambda h: h.tensor_tensor(out=T2_3, in0=bim, in1=qr_b, op=ALU.mult), reads=["Bcat", "qr"], writes=["T2"])
                P.op("dve", lambda h: h.tensor_tensor(out=T3_3, in0=bre, in1=qi_b, op=ALU.mult), reads=["Bcat", "qi", "T1"], writes=["T3"])
                P.op("dve", lambda h: h.tensor_tensor(out=T2, in0=T2, in1=T3, op=ALU.add), reads=["T2", "T3"], writes=["T2"])
                P.op("act", lambda h: h.activation(out=BB[0:64, :], in_=T1[0:64, :], func=AF.Copy), reads=["T1"], writes=["BB"])
                P.op("act", lambda h: h.activation(out=BB[64:128, :], in_=T2[64:128, :], func=AF.Copy), reads=["T2"], writes=["BB"])
                P.op("act", lambda h: h.activation(out=BBsw[0:64, :], in_=T2[0:64, :], func=AF.Copy), reads=["T2"], writes=["BBsw"])
                P.op("act", lambda h: h.activation(out=BBsw[64:128, :], in_=T1[64:128, :], func=AF.Copy), reads=["T1"], writes=["BBsw"])


            tiles = [(0, 512), (512, 512), (1024, 512), (1536, 512), (2048, NS)]
            xt_i = [0]
            subsA = []
            for (t0_, nt_) in tiles:
                if nt_ == NS:
                    subsA.append((xs, NS))
                else:
                    for sidx_ in range(4):
                        subsA.append((xp[t0_ + sidx_ * 128: t0_ + (sidx_ + 1) * 128, :], 128))

            def load_xt(k):
                if k >= len(subsA):
                    return
                src_, ns_ = subsA[k]
                P.dma(Xt[k % 2][0:ns_], src_, writes=["Xt%d" % (k % 2)], key="xt%d" % (k % 2))

            def xT_part(ti, t0, nt):
                is_s = (nt == NS)
                nsub = 1 if is_s else 4
                xT3 = xTd[ti % 2]
                xtn = "xT%d" % (ti % 2)
                for sidx in range(nsub):
                    ns = NS if is_s else 128
                    xs_slot = xt_i[0] % 2
                    xt_i[0] += 1
                    for half in range(2):
                        b = nb()

                        def tr(h, b=b, half=half, ns=ns, xs_slot=xs_slot):
                            r = None
                            for j in range(4):
                                kc = half * 4 + j
                                r = h.transpose(out=PS[b][:, j * 128: j * 128 + ns],
                                                in_=Xt[xs_slot][0:ns, kc * 128:(kc + 1) * 128],
                                                identity=ID[0:ns, 0:ns])
                            return r
                        P.op("pe", tr, reads=["Xt%d" % xs_slot, "ID"], writes=["ps%d" % b])
                        src_ps = PS[b][:, :].rearrange("p (j t) -> p j t", j=4)[:, :, 0:ns]
                        dst = xT3[:, half * 4:(half + 1) * 4, sidx * 128: sidx * 128 + ns]
                        if half == 0:
                            P.op("act", lambda h, dst=dst, src_ps=src_ps: h.activation(out=dst, in_=src_ps, func=AF.Copy),
                                 reads=["ps%d" % b], writes=[xtn])
                        else:
                            P.op("dve", lambda h, dst=dst, src_ps=src_ps: h.tensor_copy(out=dst, in_=src_ps),
                                 reads=["ps%d" % b], writes=[xtn])

                    load_xt(xt_i[0] + 1)

            def tileA(ti, t0, nt, with_xt=False, nxt=None):
                is_s = (nt == NS)
                if with_xt:
                    xT_part(ti, t0, nt)
                xT3 = xTd[ti % 2]
                xtn = "xT%d" % (ti % 2)

                def proj(ot, b):
                    def f(h):
                        r = None
                        for kc in range(8):
                            r = h.matmul(PS[b][:, 0:nt], lhsT=Win3[:, kc, ot * 128:(ot + 1) * 128],
                                         rhs=xT3[:, kc, 0:nt], start=(kc == 0), stop=(kc == 7))
                        return r
                    P.op("pe", f, reads=[xtn, "Win_%d_0" % (ot // 4), "Win_%d_1" % (ot // 4)], writes=["ps%d" % b])

                for q in range(4):
                    b = nb()
                    proj(q, b)
                    P.op("dve", lambda h, b=b, q=q: h.tensor_copy(out=uT3[:, q, t0:t0 + nt], in_=PS[b][:, 0:nt]),
                         reads=["ps%d" % b], writes=["uT"])
                for q in range(4):
                    b = nb()
                    proj(4 + q, b)
                    P.op("act", lambda h, b=b, q=q: h.activation(out=SGt[ti % 2][:, q, 0:nt], in_=PS[b][:, 0:nt], func=AF.Silu),
                         reads=["ps%d" % b], writes=["SGt%d" % (ti % 2)])
                P.dma(SGscr[:, :, t0:t0 + nt], SGt[ti % 2][:, :, 0:nt], reads=["SGt%d" % (ti % 2)], key="spillsg%d" % (ti % 2), queue="act")
                if nxt is not None:
                    xT_part(*nxt)

                def poolg(gp):
                    w = POOL_WINDOWS[gp]
                    sl = gp % 2
                    b = nb()
                    proj(8 + gp, b)
                    if not is_s:
                        P.op("act", lambda h, b=b, sl=sl: h.activation(out=PinT[sl][:, 0:nt], in_=PS[b][:, 0:nt], func=AF.Copy),
                             reads=["ps%d" % b], writes=["PinT%d" % sl])
                        if ti == 3:
                            P.op("dve", lambda h, b=b, gp=gp: h.tensor_copy(out=PLAST3[:, gp, :], in_=PS[b][:, nt - 16:nt]),
                                 reads=["ps%d" % b], writes=["PLAST"])
                    else:
                        P.op("dve", lambda h, b=b, gp=gp: h.tensor_copy(out=PinS3[:, gp, :], in_=PS[b][:, 0:nt]),
                             reads=["ps%d" % b], writes=["PinS"])
                    b2 = nb()
                    proj(12 + gp, b2)
                    P.op("act", lambda h, b2=b2, sl=sl: h.activation(out=SPG[sl][:, 0:nt], in_=PS[b2][:, 0:nt], func=AF.Silu),
                         reads=["ps%d" % b2], writes=["SPG%d" % sl])
                    if not is_s:
                        if ti == 0:
                            P.op("pool", lambda h, gp=gp: h.memset(V[gp][:, 0:15], 0.0), writes=["V%d" % gp])
                        else:
                            P.op("pool", lambda h, gp=gp: h.tensor_copy(out=V[gp][:, 0:15], in_=V[gp][:, 512:527]),
                                 reads=["V%d" % gp], writes=["V%d" % gp])
                        b3 = nb()
                        P.op("pe", lambda h, b3=b3, gp=gp, sl=sl: h.matmul(PS[b3][:, 0:nt], lhsT=Wpool3[:, gp, :], rhs=PinT[sl][:, 0:nt],
                                                                         start=True, stop=True),
                             reads=["PinT%d" % sl, "Wpool"], writes=["ps%d" % b3])
                        P.op("act", lambda h, b3=b3, gp=gp: h.activation(out=V[gp][:, 15:15 + nt], in_=PS[b3][:, 0:nt], func=AF.Copy),
                             reads=["ps%d" % b3], writes=["V%d" % gp])
                        L = 15 + nt
                        src = V[gp]
                        srcn = "V%d" % gp
                        bufs = [(W1, "W1"), (W2, "W2")]
                        nsteps = int(math.log2(w))
                        for k in range(nsteps):
                            sh = 1 << k
                            lo = (1 << (k + 1)) - 1
                            dstb, dstn = bufs[k % 2]
                            P.op("pool", lambda h, dstb=dstb, src=src, lo=lo, sh=sh, L=L:
                                 h.tensor_tensor(out=dstb[:, lo:L], in0=src[:, lo:L], in1=src[:, lo - sh:L - sh], op=ALU.add),
                                 reads=[srcn], writes=[dstn])
                            src, srcn = dstb, dstn
                        tmp = TMP[sl]
                        P.op("dve", lambda h, tmp=tmp, src=src, gp=gp, w=w: h.scalar_tensor_tensor(
                            out=tmp[:, 0:nt], in0=src[:, 15:15 + nt], scalar=1.0 / w, in1=V[gp][:, 15:15 + nt],
                            op0=ALU.mult, op1=ALU.subtract), reads=[srcn, "V%d" % gp], writes=["TMP%d" % sl])
                        if ti == 0:
                            nfix = w - 1
                            P.op("dve", lambda h, tmp=tmp, src=src, nfix=nfix: h.tensor_tensor(
                                out=tmp[:, 0:nfix], in0=src[:, 15:15 + nfix], in1=INVC[:, 0:nfix], op=ALU.mult),
                                reads=[srcn, "INVC", "TMP%d" % sl], writes=["TMP%d" % sl])
                            P.op("dve", lambda h, tmp=tmp, gp=gp, nfix=nfix: h.tensor_tensor(
                                out=tmp[:, 0:nfix], in0=tmp[:, 0:nfix], in1=V[gp][:, 15:15 + nfix], op=ALU.subtract),
                                reads=["V%d" % gp, "TMP%d" % sl], writes=["TMP%d" % sl])
                        P.op("dve", lambda h, tmp=tmp, gp=gp, sl=sl: h.scalar_tensor_tensor(
                            out=PMt[ti % 2][:, gp, 0:nt], in0=tmp[:, 0:nt], scalar=PSC[:, gp:gp + 1], in1=SPG[sl][:, 0:nt],
                            op0=ALU.mult, op1=ALU.mult), reads=["TMP%d" % sl, "SPG%d" % sl, "PSC"], writes=["PMt%d" % (ti % 2)])
                    else:
                        if gp == 0:
                            for hh in range(2):
                                bt = nb()

                                def trp(h, bt=bt, hh=hh):
                                    r = None
                                    for g2 in range(4):
                                        r = h.transpose(out=PS[bt][:, g2 * 128: g2 * 128 + 120],
                                                        in_=SPn3[0:120, hh, g2 * 128:(g2 + 1) * 128], identity=ID[0:120, 0:120])
                                    return r
                                P.op("pe", trp, reads=["SPn", "ID"], writes=["ps%d" % bt])
                                P.op("act", lambda h, bt=bt, hh=hh: h.activation(
                                    out=PfT3[:, :, hh * 120:(hh + 1) * 120],
                                    in_=PS[bt][:, :].rearrange("p (g t) -> p g t", g=4)[:, :, 0:120], func=AF.Copy),
                                    reads=["ps%d" % bt], writes=["PfT"])
                        pf = PfT3[:, gp, :].rearrange("p (b r) -> p b r", r=15)[:, :, 16 - w:15]
                        P.op("dve", lambda h, pf=pf: h.tensor_reduce(out=WSUM, in_=pf, axis=AX.X, op=ALU.add),
                             reads=["PfT"], writes=["WSUM"])
                        P.op("dve", lambda h, gp=gp: h.tensor_tensor(out=WSUM, in0=WSUM, in1=PinS3[:, gp, :], op=ALU.add),
                             reads=["PinS", "WSUM"], writes=["WSUM"])
                        P.op("dve", lambda h, gp=gp, w=w: h.scalar_tensor_tensor(
                            out=PoS, in0=WSUM, scalar=1.0 / w, in1=PinS3[:, gp, :], op0=ALU.mult, op1=ALU.subtract),
                            reads=["WSUM", "PinS"], writes=["PoS"])
                        b3 = nb()
                        P.op("pe", lambda h, b3=b3, gp=gp: h.matmul(PS[b3][:, 0:NS], lhsT=Wpool3[:, gp, :], rhs=PoS,
                                                                  start=True, stop=True),
                             reads=["PoS", "Wpool"], writes=["ps%d" % b3])
                        P.op("dve", lambda h, b3=b3, gp=gp, sl=sl: h.scalar_tensor_tensor(
                            out=PMt[ti % 2][:, gp, 0:nt], in0=PS[b3][:, 0:NS], scalar=PSC[:, gp:gp + 1], in1=SPG[sl][:, 0:nt],
                            op0=ALU.mult, op1=ALU.mult), reads=["ps%d" % b3, "SPG%d" % sl, "PSC"], writes=["PMt%d" % (ti % 2)])

                for gp_ in range(4):
                    poolg(gp_)
                P.dma(PMscr[:, :, t0:t0 + nt], PMt[ti % 2][:, :, 0:nt], reads=["PMt%d" % (ti % 2)], key="spillpm%d" % (ti % 2), queue="pool")

            load_xt(0)
            load_xt(1)
            P.dma(PSC, pool_scale.rearrange("(q p) -> p q", p=128), writes=["PSC"], key="vec", allow_slow_non_contiguous=True)

            for j, srcd in enumerate([lam_re, lam_re, lam_im, lam_im]):
                P.dma(L2[0:G, j * 64:(j + 1) * 64], srcd, writes=["L2"], key="l2")
            P.dma(Dt, log_dt.partition_broadcast(128), writes=["Dt"], key="l2")
            P.replay([P.record(weight_chain), P.record(tileA, 0, tiles[0][0], tiles[0][1], True, (1,) + tiles[1])])
            P.dma(Bcat3[0:64, :, 0:16], b_re.rearrange("g n c -> n g c"), writes=["Bcat"], key="bc", allow_slow_non_contiguous=True)
            P.dma(Bcat3[0:64, :, 16:32], b_im.rearrange("g n c -> n g c"), writes=["Bcat"], key="bc", allow_slow_non_contiguous=True)
            P.dma(Dv, ssm_d.rearrange("(q p) -> p q", p=128), writes=["Dv"], key="vec", allow_slow_non_contiguous=True)
            P.dma(GB, glu_b.rearrange("(q p) -> p q", p=128), writes=["GB"], key="vec", allow_slow_non_contiguous=True)
            P.dma(SPn3[0:120], spool.rearrange("b r c -> (b r) c").rearrange("(h x) c -> x h c", h=2),
                  writes=["SPn"], key="spn")
            P.dma(spool_o[:, 0:14, :], spool[:, 1:15, :], key="outp")
            param_head()
            tileA(1, tiles[1][0], tiles[1][1], False, (2,) + tiles[2])
            def tiles23():
                tileA(2, tiles[2][0], tiles[2][1], False, (3,) + tiles[3])
                tileA(3, tiles[3][0], tiles[3][1], False, (4,) + tiles[4])
            t23 = P.record(tiles23)
            pch = P.record(param_chain)
            merged = []
            i_ = j_ = 0
            while i_ < len(t23) or j_ < len(pch):
                for _ in range(3):
                    if i_ < len(t23):
                        merged.append(t23[i_]); i_ += 1
                if j_ < len(pch):
                    merged.append(pch[j_]); j_ += 1
            P.replay([merged])
            tileA(4, tiles[4][0], tiles[4][1], False, None)

            for which, (src3, srcn) in enumerate([(PLAST3, "PLAST"), (PinS3, "PinS")]):
                bt = nb()

                def trl(h, bt=bt, src3=src3):
                    r = None
                    for g2 in range(4):
                        r = h.transpose(out=PS[bt][0:16, g2 * 128:(g2 + 1) * 128], in_=src3[:, g2, :], identity=ID)
                    return r
                P.op("pe", trl, reads=[srcn, "ID"], writes=["ps%d" % bt])
                P.op("act", lambda h, bt=bt: h.activation(out=ROW[0:16, :], in_=PS[bt][0:16, :], func=AF.Copy),
                     reads=["ps%d" % bt], writes=["ROW"])
                if which == 0:
                    P.dma(ppool[0:15, :], ROW[1:16, :], reads=["ROW"], key="outp")
                else:
                    P.dma(spool_o[:, 14, :], ROW[0:16, :], reads=["ROW"], key="outp")
            P.barrier()
            P.emit(blk)
        if KSTOP == 1:
            return nc

        CAqball = [b16([128, 8 * 17 * 16], 2, 3) for _ in range(4)]
        CAp0 = b16([128, G * 32], 2, 3)
        CAp03 = CAp0.rearrange("p (g m) -> p g m", g=G)
        H0n = f32([128, 4 * 128], 3, 3); H0n3 = H0n.rearrange("p (q n) -> p q n", q=4)
        H0n2 = f32([128, 4 * 128], 3, 3); H0n23 = H0n2.rearrange("p (q n) -> p q n", q=4)
        with ExitStack() as s2:
            blk = s2.enter_context(nc.Block())
            BAqd = [f32([128, 16 * 128], 2, 2) for _ in range(2)]; BAtd = [f32([128, 16 * 128], 2, 2) for _ in range(2)]
            Cnat = f32([128, 4 * 128], 2, 2); Cnat2 = f32([128, 4 * 128], 2, 2)
            Cnat3 = Cnat.rearrange("p (q n) -> p q n", q=4); Cnat23 = Cnat2.rearrange("p (q n) -> p q n", q=4)
            CC = f32([128, 4 * 128], 2, 2); CCsw = f32([128, 4 * 128], 2, 2)
            CAq = f32([128, 8 * 17 * 16], 2, 2); CAt = f32([128, 8 * 17 * 16], 2, 2)
            CAq4 = CAq.rearrange("p (g t c) -> p g t c", g=8, t=17); CAt4 = CAt.rearrange("p (g t c) -> p g t c", g=8, t=17)
            BApT = b16([128, 4 * 16 * 2 * 128], 2, 2)
            BAp5 = BApT.rearrange("p (q t e n) -> p q t e n", q=4, t=16, e=2)

            if KSTOP == 2 and KSUB == 2:
                P.barrier(); P.emit(blk); return nc
            def ba_chain(q):
                BAq = BAqd[q % 2]; BAt = BAtd[q % 2]
                nq = "BAq%d" % (q % 2); nt_ = "BAt%d" % (q % 2)
                BAq4 = BAq.rearrange("p (t g c) -> p t g c", t=16, g=8)
                BAt4 = BAt.rearrange("p (t g c) -> p t g c", t=16, g=8)
                pr_b = raw(PR, [[1, 16], [17, 8], [0, 16]], extra_off=q * 8 * 17)
                pis_b = raw(PIs, [[1, 16], [17, 8], [0, 16]], extra_off=q * 8 * 17)
                bb_b = raw(BB, [[0, 16], [16, 8], [1, 16]], extra_off=q * 8 * 16)
                bbsw_b = raw(BBsw, [[0, 16], [16, 8], [1, 16]], extra_off=q * 8 * 16)
                P.op("dve", lambda h, pr_b=pr_b, bb_b=bb_b, BAq4=BAq4: h.tensor_tensor(out=BAq4, in0=pr_b, in1=bb_b, op=ALU.mult),
                     reads=["PR", "BB"], writes=[nq])
                P.op("pool", lambda h, pis_b=pis_b, bbsw_b=bbsw_b, BAt4=BAt4: h.tensor_tensor(out=BAt4, in0=pis_b, in1=bbsw_b, op=ALU.mult),
                     reads=["PIs", "BBsw"], writes=[nt_])
                P.op("dve", lambda h, BAq=BAq, BAt=BAt: h.tensor_tensor(out=BAq, in0=BAq, in1=BAt, op=ALU.add), reads=[nq, nt_], writes=[nq])
                for tq4 in range(4):
                    bt = nb()

                    def trb(h, bt=bt, tq4=tq4, BAq=BAq):
                        r = None
                        for j in range(4):
                            r = h.transpose(out=PS[bt][:, j * 128:(j + 1) * 128], in_=BAq[:, (tq4 * 4 + j) * 128:(tq4 * 4 + j + 1) * 128],
                                            identity=ID)
                        return r
                    P.op("pe", trb, reads=[nq, "ID"], writes=["ps%d" % bt])
                    psv = PS[bt][:, :].rearrange("p (t n) -> p t n", t=4)
                    P.op("act", lambda h, psv=psv, q=q, tq4=tq4: h.activation(
                        out=BAp5[:, q, tq4 * 4:(tq4 + 1) * 4, 0, :], in_=psv, func=AF.Copy, scale=M0[:, 0:1]),
                        reads=["ps%d" % bt, "M0"], writes=["BAp%d" % q])
                    P.op("act", lambda h, psv=psv, q=q, tq4=tq4: h.activation(
                        out=BAp5[:, q, tq4 * 4:(tq4 + 1) * 4, 1, :], in_=psv, func=AF.Copy, scale=M1[:, 0:1]),
                        reads=["ps%d" % bt, "M1"], writes=["BAp%d" % q])

            P.replay([P.record(ba_chain, 0), P.record(ba_chain, 1)])
            P.replay([P.record(ba_chain, 2), P.record(ba_chain, 3)])
            P.op("pool", lambda h: h.memset(CAp0, 0.0), writes=["CAp0"])
            P.dma(Cnat3[:, :, 0:64], c_re.rearrange("(q gl) co n -> (gl co) q n", q=4), writes=["Cnat"], key="cn")
            P.dma(Cnat3[:, :, 64:128], c_im.rearrange("(q gl) co n -> (gl co) q n", q=4), writes=["Cnat"], key="cn")
            P.dma(Cnat23[:, :, 0:64], c_im.rearrange("(q gl) co n -> (gl co) q n", q=4), writes=["Cnat2"], key="cn")
            P.dma(Cnat23[:, :, 64:128], c_re.rearrange("(q gl) co n -> (gl co) q n", q=4), writes=["Cnat2"], key="cn")
            for (srcC, dstC, nm_s, nm_d, sgn) in [(Cnat3, CC, "Cnat", "CC", True), (Cnat23, CCsw, "Cnat2", "CCsw", False)]:
                bt = nb()

                def trc(h, bt=bt, srcC=srcC):
                    r = None
                    for q in range(4):
                        r = h.transpose(out=PS[bt][:, q * 128:(q + 1) * 128], in_=srcC[:, q, :], identity=ID)
                    return r
                P.op("pe", trc, reads=[nm_s, "ID"], writes=["ps%d" % bt])
                if sgn:
                    P.op("dve", lambda h, bt=bt, dstC=dstC: h.tensor_scalar(out=dstC, in0=PS[bt][:, :], scalar1=NSGN[:, 0:1], scalar2=None, op0=ALU.mult),
                         reads=["ps%d" % bt, "NSGN"], writes=[nm_d])
                else:
                    P.op("dve", lambda h, bt=bt, dstC=dstC: h.tensor_copy(out=dstC, in_=PS[bt][:, :]), reads=["ps%d" % bt], writes=[nm_d])
            def c1a_chain(q):
                g0 = q * 8
                pr_b = raw(PR, [[17, 8], [1, 17], [0, 16]], extra_off=g0 * 17)
                pi_b = raw(PI, [[17, 8], [1, 17], [0, 16]], extra_off=g0 * 17)
                cc_b = raw(CC, [[16, 8], [0, 17], [1, 16]], extra_off=q * 128)
                ccsw_b = raw(CCsw, [[16, 8], [0, 17], [1, 16]], extra_off=q * 128)
                P.op("dve", lambda h, pr_b=pr_b, cc_b=cc_b: h.tensor_tensor(out=CAq4, in0=pr_b, in1=cc_b, op=ALU.mult),
                     reads=["PR", "CC", "CAq"], writes=["CAq"])
                P.op("pool", lambda h, pi_b=pi_b, ccsw_b=ccsw_b: h.tensor_tensor(out=CAt4, in0=pi_b, in1=ccsw_b, op=ALU.mult),
                     reads=["PI", "CCsw"], writes=["CAt"])
                P.op("dve", lambda h: h.tensor_tensor(out=CAq, in0=CAq, in1=CAt, op=ALU.subtract), reads=["CAq", "CAt"], writes=["CAq"])
                P.op("act", lambda h: h.activation(out=CAqball[q], in_=CAq, func=AF.Copy), reads=["CAq"], writes=["CAqb%d" % q])
                for e in range(2):
                    co = raw(CAp0, [[64, 4], [1, 16]], extra_off=(g0 + e) * 32 + e * 16)
                    ci = raw(CAq, [[2 * 272, 4], [1, 16]], extra_off=e * 272)
                    P.op("act", lambda h, co=co, ci=ci: h.activation(out=co, in_=ci, func=AF.Copy), reads=["CAq"], writes=["CAp0"])
            for q_ in range(4):
                c1a_chain(q_)
            if KSTOP == 2 and KSUB == 3:
                P.barrier(); P.emit(blk); return nc
            uTc = uT3[:, :, 0:SEQ].rearrange("p q (c i) -> p q c i", i=TCH)
            sbanks = [[nb(), nb()] for _ in range(4)]
            allb = ["ps%d" % x for pair in sbanks for x in pair]

            def smm(h):
                r = None
                for q in range(4):
                    for e in range(2):
                        k = (q % 2) * 2 + e
                        for i in range(TCH):
                            for rr in range(4):
                                bt = sbanks[rr][q // 2]
                                r = h.matmul(PS[bt][:, k * 128:(k + 1) * 128],
                                             lhsT=BAp5[32 * rr:32 * rr + 32, q, 15 - i, e, :],
                                             rhs=uTc[32 * rr:32 * rr + 32, q, :, i],
                                             start=(i == 0), stop=(i == TCH - 1), tile_position=(32 * rr, 0))
                return r
            P.op("pe", smm, reads=["BAp0", "BAp1", "BAp2", "BAp3", "uT"], writes=allb)
            for rr in range(4):
                for qh in range(2):
                    bt = sbanks[rr][qh]
                    so = raw(S_sb, [[1024, 2], [128, 2], [1, 128]], extra_off=qh * 2048 + rr * 256)
                    si = PS[bt][:, :].rearrange("p (a e c) -> p a e c", a=2, e=2)
                    if (rr + qh) % 2 == 0:
                        P.op("act", lambda h, so=so, si=si: h.activation(out=so, in_=si, func=AF.Copy), reads=["ps%d" % bt], writes=["S"])
                    else:
                        P.op("dve", lambda h, so=so, si=si: h.tensor_copy(out=so, in_=si), reads=["ps%d" % bt], writes=["S"])
            if KSTOP == 2 and KSUB == 4:
                P.barrier(); P.emit(blk); return nc
            xbanks = [nb() for _ in range(4)]

            def xsm(h):
                r = None
                for q in range(4):
                    for e in range(2):
                        for rr in range(4):
                            k = q * 2 + e
                            r = h.matmul(PS[xbanks[rr]][:, k * NS:(k + 1) * NS], lhsT=BAp5[32 * rr:32 * rr + 32, q, 0, e, :],
                                         rhs=uT3[32 * rr:32 * rr + 32, q, SEQ:SEQ + NS], start=True, stop=True,
                                         tile_position=(32 * rr, 0))
                return r
            P.op("pe", xsm, reads=["BAp0", "BAp1", "BAp2", "BAp3", "uT"], writes=["ps%d" % x for x in xbanks])
            if KSTOP == 2 and KSUB == 5:
                P.barrier(); P.emit(blk); return nc
            for rr in range(4):
                xo = raw(Xs_sb, [[128, 4], [16, 2], [1, 16]], extra_off=rr * 32)
                xi = PS[xbanks[rr]][:, 0:8 * NS].rearrange("p (q e b) -> p q e b", q=4, e=2)
                P.op("dve", lambda h, xo=xo, xi=xi: h.tensor_copy(out=xo, in_=xi), reads=["ps%d" % xbanks[rr]], writes=["Xs"])
            P.barrier()
            P.emit(blk)
        if KSTOP == 2:
            return nc

        with ExitStack() as s3:
            blk = s3.enter_context(nc.Block())
            TH16 = f32([128, G], 3, 3); TH16r = f32([128, G], 3, 3); RHO = f32([128, G], 3, 3); tt0 = f32([128, G], 3, 3)
            MAGICT = f32([128, 1], 3, 3); NMAGICT = f32([128, 1], 3, 3)
            GN = 8
            NB = GN * NCH
            BfA = [f32([128, NB], 3, 3) for _ in range(6)]
            BfB = [f32([128, NB], 3, 3) for _ in range(3)] + [A16.alloc(2 * NB, 3, 3).bitcast(F32) for _ in range(3)]
            hb_ctr = [0]
            Hb = b16([128, G * 129], 3, 3)
            Hb3 = Hb.rearrange("p (g c) -> p g c", g=G)
            Hfin = f32([128, G], 3, 3)
            HfT = f32([128, 128], 3, 3)
            Bpad = b16([128, 8 * 128], 3, 3)
            Bpad3 = Bpad.rearrange("p (g m) -> p g m", g=8)
            Ddiag = f32([128, 4 * 128], 3, 3); Ddiag3 = Ddiag.rearrange("p (q m) -> p q m", q=4)
            Ddb = b16([128, 4 * 128], 3, 3); Ddb3 = Ddb.rearrange("p (q m) -> p q m", q=4)
            AIs = f32([128, G], 3, 3)
            Hn = f32([128, G * NS], 3, 3); Hn3 = Hn.rearrange("p (g b) -> p g b", g=G)
            Hn2 = f32([128, G * NS], 3, 3)
            Hnb = b16([128, G * NS], 3, 3); Hnb3 = Hnb.rearrange("p (g b) -> p g b", g=G)
            Hout = f32([128, 4 * 128], 3, 3); Hout3 = Hout.rearrange("p (q n) -> p q n", q=4)

            def reduce_angle3(dst, src, tmpb, n_dst, n_src, n_tmp, eng="dve"):
                P.op("act", lambda h: h.activation(out=tmpb, in_=src, func=AF.Identity, scale=1.0 / TWO_PI, bias=MAGICT),
                     reads=[n_src, "MAGICT"], writes=[n_tmp])
                P.op("act", lambda h: h.activation(out=tmpb, in_=tmpb, func=AF.Identity, scale=1.0, bias=NMAGICT),
                     reads=[n_tmp, "MAGICT"], writes=[n_tmp])
                P.op(eng, lambda h: h.scalar_tensor_tensor(out=dst, in0=tmpb, scalar=-TWO_PI, in1=src, op0=ALU.mult, op1=ALU.add),
                     reads=[n_tmp, n_src], writes=[n_dst])

            P.op("pool", lambda h: h.memset(MAGICT, MAGIC), writes=["MAGICT"])
            P.op("pool", lambda h: h.memset(NMAGICT, -MAGIC), writes=["MAGICT"])
            P.op("dve", lambda h: h.tensor_scalar(out=TH16, in0=THr, scalar1=float(TCH), scalar2=None, op0=ALU.mult), reads=["THr"], writes=["TH16"])
            reduce_angle3(TH16r, TH16, tt0, "TH16r", "TH16", "tt0")
            P.op("act", lambda h: h.activation(out=RHO, in_=LR, func=AF.Exp, scale=float(TCH)), reads=["LR"], writes=["RHO"])
            P.op("pool", lambda h: h.memset(Hb, 0.0), writes=["Hb%d" % (4 * i_) for i_ in range(8)])
            for q in range(4):
                P.op("pool", lambda h, q=q: h.tensor_scalar(out=Ddiag3[:, q, :], in0=ID, scalar1=Dv[:, q:q + 1], scalar2=None, op0=ALU.mult),
                     reads=["ID", "Dv"], writes=["Ddiag"])
            P.op("pool", lambda h: h.tensor_copy(out=Ddb, in_=Ddiag), reads=["Ddiag"], writes=["Ddb"])

            for gl in range(8):
                sre_v = sre.rearrange("b (q gl) n -> gl b q n", q=4)[gl]
                sim_v = sim.rearrange("b (q gl) n -> gl b q n", q=4)[gl]
                P.dma(H0n3[gl * NS:(gl + 1) * NS, :, 0:64], sre_v, writes=["H0n"], key="h0")
                P.dma(H0n3[gl * NS:(gl + 1) * NS, :, 64:128], sim_v, writes=["H0n"], key="h0")
                P.dma(H0n23[gl * NS:(gl + 1) * NS, :, 0:64], sim_v, writes=["H0n2"], key="h0")
                P.dma(H0n23[gl * NS:(gl + 1) * NS, :, 64:128], sre_v, writes=["H0n2"], key="h0")
            def hb_chain(q, hb):
                g0 = q * 8
                if True:
                    g0h = g0
                    bs = q % 2
                    Bf = BfA if bs == 0 else BfB
                    PH, PHr, SINP, COSP, SMS, SW = Bf
                    Z, ZT = Bf[0], Bf[1]
                    B0, B1, B2, B3, B4, B5 = ["B%d_%d" % (bs, i) for i in range(6)]
                    th_b = raw(TH16r, [[1, GN], [0, NCH]], extra_off=g0h)
                    ci_b = raw(CIDX, [[0, GN], [1, NCH]])
                    PH3 = PH.rearrange("p (g c) -> p g c", g=GN)
                    P.op("pool", lambda h, th_b=th_b, ci_b=ci_b, PH3=PH3: h.tensor_tensor(out=PH3, in0=th_b, in1=ci_b, op=ALU.mult),
                         reads=["TH16r", "CIDX", B0], writes=[B0])
                    reduce_angle3(PHr, PH, SW, B1, B0, B5, eng="dve")
                    P.op("act", lambda h, SINP=SINP, PHr=PHr: h.activation(out=SINP, in_=PHr, func=AF.Sin, scale=0.999999), reads=[B1], writes=[B2])
                    P.op("act", lambda h, PH=PH, PHr=PHr: h.activation(out=PH, in_=PHr, func=AF.Abs), reads=[B1], writes=[B0])
                    P.op("act", lambda h, COSP=COSP, PH=PH: h.activation(out=COSP, in_=PH, func=AF.Sin, bias=HALFPI, scale=-0.999999),
                         reads=[B0, "HALFPI"], writes=[B3])
                    P.op("pool", lambda h, SMS=SMS, SINP=SINP: h.tensor_scalar(out=SMS, in0=SINP, scalar1=NSGN[:, 0:1], scalar2=1.0, op0=ALU.mult, op1=ALU.mult),
                         reads=[B2, "NSGN"], writes=[B4])
                    Sq = S_sb[:, g0h * NCH:(g0h + GN) * NCH]
                    sn = ["S"]
                    P.op("act", lambda h, Sq=Sq, SW=SW: h.activation(out=SW[0:64, :], in_=Sq[64:128, :], func=AF.Copy), reads=sn, writes=[B5])
                    P.op("act", lambda h, Sq=Sq, SW=SW: h.activation(out=SW[64:128, :], in_=Sq[0:64, :], func=AF.Copy), reads=sn, writes=[B5])
                    P.op("dve", lambda h, Sq=Sq, Z=Z, COSP=COSP: h.tensor_tensor(out=Z, in0=Sq, in1=COSP, op=ALU.mult), reads=sn + [B3, B0], writes=[B0])
                    P.op("pool", lambda h, ZT=ZT, SW=SW, SMS=SMS: h.tensor_tensor(out=ZT, in0=SW, in1=SMS, op=ALU.mult), reads=[B5, B4, B1], writes=[B1])
                    P.op("dve", lambda h, Z=Z, ZT=ZT: h.tensor_tensor(out=Z, in0=Z, in1=ZT, op=ALU.add), reads=[B0, B1], writes=[B0])
                    for gl in range(GN):
                        rho_b = raw(RHO, [[0, NCH]], extra_off=g0h + gl)
                        P.op("dve", lambda h, gl=gl, rho_b=rho_b, Z=Z, ZT=ZT: h.tensor_tensor_scan(
                            out=ZT[:, gl * NCH:(gl + 1) * NCH], data0=rho_b, data1=Z[:, gl * NCH:(gl + 1) * NCH],
                            initial=0.0, op0=ALU.mult, op1=ALU.add), reads=[B0, "RHO", B1], writes=[B1])
                    P.op("act", lambda h, SW=SW, ZT=ZT: h.activation(out=SW[0:64, :], in_=ZT[64:128, :], func=AF.Copy), reads=[B1, B5], writes=[B5])
                    P.op("act", lambda h, SW=SW, ZT=ZT: h.activation(out=SW[64:128, :], in_=ZT[0:64, :], func=AF.Copy), reads=[B1, B5], writes=[B5])
                    P.op("dve", lambda h, Z=Z, ZT=ZT, COSP=COSP: h.tensor_tensor(out=Z, in0=ZT, in1=COSP, op=ALU.mult), reads=[B1, B3, B0], writes=[B0])
                    P.op("pool", lambda h, ZT=ZT, SW=SW, SMS=SMS: h.tensor_tensor(out=ZT, in0=SW, in1=SMS, op=ALU.mult), reads=[B5, B4, B0, B1], writes=[B1])
                    P.op("dve", lambda h, Z=Z, ZT=ZT: h.tensor_tensor(out=Z, in0=Z, in1=ZT, op=ALU.subtract), reads=[B0, B1], writes=[B0])
                    Z3 = Z.rearrange("p (g c) -> p g c", g=GN)
                    P.op("act", lambda h, g0h=g0h, Z3=Z3: h.activation(out=Hb3[:, g0h:g0h + GN, 1:129], in_=Z3, func=AF.Copy), reads=[B0], writes=["Hb%d" % g0h, "Hb%d" % (g0h + 4)])
                    P.op("pool", lambda h, g0h=g0h, Z3=Z3: h.tensor_copy(out=Hfin[:, g0h:g0h + GN], in_=Z3[:, :, NCH - 1]), reads=[B0], writes=["Hfin%d" % g0h, "Hfin%d" % (g0h + 4)])

            def c1b_chain(q):
                g0 = q * 8
                if q == 0:
                    P.op("pool", lambda h: h.memset(Bpad, 0.0), writes=["Bpad"])
                bdiag = raw(Bpad, [[128 + 16, 8], [1, 16]])
                bbq = raw(BB, [[16, 8], [1, 16]], extra_off=g0 * 16)
                P.op("pool", lambda h, bdiag=bdiag, bbq=bbq: h.tensor_copy(out=bdiag, in_=bbq), reads=["BB", "Bpad"], writes=["Bpad"])
                for tq4 in range(4):
                    bt = nb()

                    def kmm(h, bt=bt, tq4=tq4, q=q):
                        r = None
                        psv = PS[bt][:, :].rearrange("p (t g c) -> p t g c", t=4, g=8)
                        for gl in range(8):
                            r = h.matmul(psv[:, :, gl, :], lhsT=Bpad3[:, gl, :], rhs=CAqball[q].rearrange("p (g t c) -> p g t c", g=8, t=17)[:, gl, tq4 * 4:(tq4 + 1) * 4, :],
                                         start=True, stop=True)
                        return r
                    P.op("pe", kmm, reads=["Bpad", "CAqb%d" % q], writes=["ps%d" % bt])
                    if tq4 == 0:
                        P.op("dve", lambda h, bt=bt, q=q: h.tensor_tensor(out=Kb4[:, q, 0, :], in0=PS[bt][:, 0:128], in1=Ddiag3[:, q, :], op=ALU.add),
                             reads=["ps%d" % bt, "Ddiag"], writes=["Kb"])
                        P.op("act", lambda h, bt=bt, q=q: h.activation(out=Kb4[:, q, 1:4, :], in_=PS[bt][:, 128:512].rearrange("p (t m) -> p t m", t=3), func=AF.Copy),
                             reads=["ps%d" % bt], writes=["Kb"])
                    else:
                        P.op("act", lambda h, bt=bt, q=q, tq4=tq4: h.activation(
                            out=Kb4[:, q, tq4 * 4:(tq4 + 1) * 4, :], in_=PS[bt][:, :].rearrange("p (t m) -> p t m", t=4), func=AF.Copy),
                            reads=["ps%d" % bt], writes=["Kb"])
            def c2_chain(q):
                g0 = q * 8
                for gp2 in range(4):
                    bt = nb()

                    def ymm(h, bt=bt, gp2=gp2, g0=g0, q=q):
                        r = None
                        for k in range(2):
                            gl = gp2 * 2 + k
                            r = h.matmul(PS[bt][:, k * 256:(k + 1) * 256], lhsT=Hb3[:, g0 + gl, 0:128],
                                         rhs=CAqball[q].rearrange("p (g t c) -> p g t c", g=8, t=17)[:, gl, 1:17, :], start=True, stop=True)
                        return r
                    P.op("pe", ymm, reads=["Hb%d" % g0, "Hb%d" % (g0 + 4), "CAqb%d" % q], writes=["ps%d" % bt])
                    ga = g0 + gp2 * 2
                    yo = raw(Yi, [[16, 2], [128, 16], [1, 16]], extra_off=q * 2048 + (gp2 * 2) * 16)
                    yin = PS[bt][:, :].rearrange("p (k j c) -> p k j c", k=2, j=16)
                    P.op("act", lambda h, yo=yo, yin=yin: h.activation(out=yo, in_=yin, func=AF.Copy),
                         reads=["ps%d" % bt], writes=["Yi"])

            def sample_p1():
                bt0 = nb(); bt1 = nb()
                for (btx, srcH, nm) in [(bt0, H0n3, "H0n"), (bt1, H0n23, "H0n2")]:
                    def trh(h, btx=btx, srcH=srcH):
                        r = None
                        for q in range(4):
                            r = h.transpose(out=PS[btx][:, q * 128:(q + 1) * 128], in_=srcH[:, q, :], identity=ID)
                        return r
                    P.op("pe", trh, reads=[nm, "ID"], writes=["ps%d" % btx])
                P.op("dve", lambda h: h.tensor_scalar(out=AIs, in0=PI3[:, :, 1], scalar1=SGN[:, 0:1], scalar2=None, op0=ALU.mult),
                     reads=["PI", "SGN"], writes=["AIs"])
                ar_b = raw(PR, [[17, G], [0, NS]], extra_off=1)
                ais_b = raw(AIs, [[1, G], [0, NS]])
                ps0v = PS[bt0][:, :].rearrange("p (g b) -> p g b", g=G)
                ps1v = PS[bt1][:, :].rearrange("p (g b) -> p g b", g=G)
                Hn2_3 = Hn2.rearrange("p (g b) -> p g b", g=G)
                P.op("dve", lambda h: h.tensor_tensor(out=Hn3, in0=ps0v, in1=ar_b, op=ALU.mult), reads=["ps%d" % bt0, "PR"], writes=["Hn"])
                P.op("dve", lambda h: h.tensor_tensor(out=Hn2_3, in0=ps1v, in1=ais_b, op=ALU.mult), reads=["ps%d" % bt1, "AIs"], writes=["Hn2"])
                P.op("dve", lambda h: h.tensor_tensor(out=Hn, in0=Hn, in1=Hn2, op=ALU.add), reads=["Hn", "Hn2"], writes=["Hn"])
                P.op("dve", lambda h: h.tensor_tensor(out=Hn, in0=Hn, in1=Xs_sb, op=ALU.add), reads=["Hn", "Xs"], writes=["Hn"])
                P.op("act", lambda h: h.activation(out=Hnb, in_=Hn, func=AF.Copy), reads=["Hn"], writes=["Hnb"])
                btA = nb()

                def tro(h, btA=btA):
                    r = None
                    for q in range(4):
                        r = h.transpose(out=PS[btA][:, q * 128:(q + 1) * 128], in_=Hn[:, q * 128:(q + 1) * 128], identity=ID)
                    return r
                P.op("pe", tro, reads=["Hn", "ID"], writes=["ps%d" % btA])
                P.op("act", lambda h, btA=btA: h.activation(out=Hout, in_=PS[btA][:, :], func=AF.Copy), reads=["ps%d" % btA], writes=["Hout"])
                for gl in range(8):
                    sre_ov = sre_o.rearrange("b (q gl) n -> gl b q n", q=4)[gl]
                    sim_ov = sim_o.rearrange("b (q gl) n -> gl b q n", q=4)[gl]
                    P.dma(sre_ov, Hout3[gl * NS:(gl + 1) * NS, :, 0:64], reads=["Hout"], key="outs")
                    P.dma(sim_ov, Hout3[gl * NS:(gl + 1) * NS, :, 64:128], reads=["Hout"], key="outs")

            aux = P.record(c1b_chain, 0) + P.record(c1b_chain, 1)
            P.replay([P.record(hb_chain, 0, 0), P.record(hb_chain, 1, 0), aux])
            aux = P.record(c2_chain, 0) + P.record(c1b_chain, 2) + P.record(c2_chain, 1) + P.record(c1b_chain, 3) + P.record(sample_p1)
            P.replay([P.record(hb_chain, 2, 0), P.record(hb_chain, 3, 0), aux])
            c2_chain(2)
            c2_chain(3)

            bt = nb()
            P.op("pe", lambda h, bt=bt: h.transpose(out=PS[bt][0:G, 0:128], in_=Hfin, identity=ID), reads=["Hfin%d" % (4 * i_) for i_ in range(8)] + ["ID"], writes=["ps%d" % bt])
            P.op("act", lambda h, bt=bt: h.activation(out=HfT[0:G, :], in_=PS[bt][0:G, 0:128], func=AF.Copy), reads=["ps%d" % bt], writes=["HfT"])
            P.dma(pre, HfT[0:G, 0:64], reads=["HfT"], key="outs")
            P.dma(pim, HfT[0:G, 64:128], reads=["HfT"], key="outs")

            bt = nb()

            def ysm(h, bt=bt):
                r = None
                for q in range(4):
                    h.matmul(PS[bt][:, q * NS:(q + 1) * NS], lhsT=Ddb3[:, q, :], rhs=uT3[:, q, SEQ:SEQ + NS], start=True, stop=False)
                    for gl in range(8):
                        g = q * 8 + gl
                        rr = gl // 2
                        r = h.matmul(PS[bt][32 * rr:32 * rr + 32, q * NS:(q + 1) * NS], lhsT=CAp03[:, g, :], rhs=Hnb3[:, g, :],
                                     start=False, stop=(gl == 7), tile_position=(0, 32 * rr))
                return r
            P.op("pe", ysm, reads=["Ddb", "uT", "CAp0", "Hnb"], writes=["ps%d" % bt])
            P.op("act", lambda h, bt=bt: h.activation(out=SYs, in_=PS[bt][:, 0:4 * NS], func=AF.Gelu_apprx_tanh), reads=["ps%d" % bt], writes=["SYs"])
            P.barrier()
            P.emit(blk)
        if KSTOP == 3:
            return nc

        with ExitStack() as s4:
            blk = s4.enter_context(nc.Block())
            Wout = b16([128, 8 * 1024], 4, 4); Wout3 = Wout.rearrange("p (k f) -> p k f", k=8)
            Wglu = b16([128, 4 * 512], 4, 4); Wglu3 = Wglu.rearrange("p (k f) -> p k f", k=4)
            Gam = f32([128, 1024], 4, 4); Bet = f32([128, 1024], 4, 4)
            STG4 = [f32([128, 1024], 4, 4) for _ in range(2)]
            Xt4 = [f32([128, 1024], 4, 4) for _ in range(2)]
            Hs4 = [f32([128, 1024], 4, 4) for _ in range(4)]
            Hs4n = ["Hs0", "Hs1", "Hs2", "Hs3"]
            SYd = [b16([128, 4 * 512], 4, 4).rearrange("p (q t) -> p q t", q=4),
                   A32.alloc(1024, 4, 4).bitcast(BF16).rearrange("p (q t) -> p q t", q=4)]
            MXd = [b16([128, 4 * 512], 4, 4).rearrange("p (q t) -> p q t", q=4),
                   A32.alloc(1024, 4, 4).bitcast(BF16).rearrange("p (q t) -> p q t", q=4)]
            SGl = [b16([128, 4 * 512], 4, 4).rearrange("p (q t) -> p q t", q=4) for _ in range(2)]
            PMl = [b16([128, 4 * 512], 4, 4).rearrange("p (q t) -> p q t", q=4) for _ in range(2)]
            SIG = [f32([128, 512], 4, 4) for _ in range(2)]
            T1b = [f32([128, 512], 4, 4) for _ in range(2)]
            STATSd = [f32([128, 12], 4, 4) for _ in range(2)]; MVd = [f32([128, 2], 4, 4) for _ in range(2)]
            RSTDd = [f32([128, 1], 4, 4) for _ in range(2)]

            for kc in range(4):
                P.dma(Wglu3[:, kc, :], glu_w[kc * 128:(kc + 1) * 128, :], writes=["Wglu"], key="wglu", queue="pool")
            for kc in range(8):
                P.dma(Wout3[:, kc, :], w_out[kc * 128:(kc + 1) * 128, :], reads=["Wglu"], writes=["Wout"], key="wout", queue="pool")
            P.dma(Gam, ln_g.partition_broadcast(128), writes=["Gam"], key="gb")
            P.dma(Bet, ln_b.partition_broadcast(128), writes=["Bet"], key="gb")

            uTc = uT3[:, :, 0:SEQ].rearrange("p q (c i) -> p q c i", i=TCH)
            tiles = [(0, 512), (512, 512), (1024, 512), (1536, 512), (2048, NS)]
            sub_i = [0]
            def front(ti, t0, nt):
                is_s = (nt == NS)
                c0 = t0 // TCH
                ncl = nt // TCH
                par = ti % 2
                SY3 = SYd[par]; MX3 = MXd[par]
                syn = ["SY%d_%d" % (par, q) for q in range(4)]
                mxn = "MX%d" % par
                if not is_s:
                    for q in range(4):
                        bt = nb()

                        def ymm4(h, bt=bt, q=q, c0=c0, ncl=ncl):
                            r = None
                            psv = PS[bt][:, :].rearrange("p (c j) -> p c j", j=TCH)
                            for tau in range(TCH):
                                r = h.matmul(psv[:, :, tau:TCH], lhsT=Kb4[:, q, tau, :], rhs=uTc[:, q, c0:c0 + ncl, 0:TCH - tau],
                                             start=(tau == 0), stop=False)
                            for j in range(TCH):
                                lw = Yi[:, q * 2048 + j * 128: q * 2048 + (j + 1) * 128]
                                r = h.matmul(psv[:, :, j], lhsT=lw, rhs=IDb[:, c0:c0 + ncl], start=False, stop=(j == TCH - 1))
                            return r
                        P.op("pe", ymm4, reads=["Kb", "uT", "Yi", "IDb"], writes=["ps%d" % bt])
                        P.op("act", lambda h, bt=bt, q=q: h.activation(out=SY3[:, q, 0:nt], in_=PS[bt][:, 0:nt], func=AF.Gelu_apprx_tanh),
                             reads=["ps%d" % bt], writes=[syn[q]])
                else:
                    P.op("pool", lambda h: h.tensor_copy(out=SY3[:, :, 0:NS], in_=SYs3), reads=["SYs"] + syn, writes=syn)
                for q2 in range(4):
                    bt = nb()
                    sl = q2 % 2

                    def gmm(h, bt=bt, q2=q2):
                        r = None
                        for kc in range(4):
                            r = h.matmul(PS[bt][:, 0:nt], lhsT=Wglu3[:, kc, q2 * 128:(q2 + 1) * 128], rhs=SY3[:, kc, 0:nt],
                                         start=(kc == 0), stop=(kc == 3))
                        return r
                    P.op("pe", gmm, reads=["Wglu"] + syn, writes=["ps%d" % bt])
                    P.op("act", lambda h, bt=bt, q2=q2, sl=sl: h.activation(out=SIG[sl][:, 0:nt], in_=PS[bt][:, 0:nt], func=AF.Sigmoid,
                                                                            bias=GB[:, q2:q2 + 1]),
                         reads=["ps%d" % bt, "GB"], writes=["SIG%d" % sl])
                    P.op("dve", lambda h, q2=q2, sl=sl: h.tensor_tensor(out=T1b[sl][:, 0:nt], in0=SIG[sl][:, 0:nt], in1=SY3[:, q2, 0:nt], op=ALU.mult),
                         reads=["SIG%d" % sl, syn[q2]], writes=["T1b%d" % sl])
                    P.op("dve", lambda h, q2=q2, sl=sl: h.tensor_tensor(out=MX3[:, q2, 0:nt], in0=T1b[sl][:, 0:nt], in1=SGl[par][:, q2, 0:nt], op=ALU.mult),
                         reads=["T1b%d" % sl, "SGl%d" % par], writes=[mxn])

            def back(ti, t0, nt):
                is_s = (nt == NS)
                par = ti % 2
                MX3 = MXd[par]
                mxn = "MX%d" % par
                nsub = 1 if is_s else 4
                def one_sub(sidx):
                    ns = NS if is_s else 128
                    sl = sub_i[0] % 2
                    hs = sub_i[0] % 4
                    HsT = Hs4[hs]; hsn = Hs4n[hs]
                    sub_i[0] += 1
                    STATS = STATSd[sl]; MV = MVd[sl]; RSTD = RSTDd[sl]
                    for half in range(2):
                        bt = nb()

                        def omm(h, bt=bt, half=half, sidx=sidx, ns=ns):
                            r = None
                            for kc in range(8):
                                if kc < 4:
                                    lw = MX3[:, kc, sidx * 128: sidx * 128 + ns]
                                else:
                                    lw = PMl[par][:, kc - 4, sidx * 128: sidx * 128 + ns]
                                r = h.matmul(PS[bt][0:ns, :], lhsT=lw, rhs=Wout3[:, kc, half * 512:(half + 1) * 512],
                                             start=(kc == 0), stop=(kc == 7))
                            return r
                        P.op("pe", omm, reads=[mxn, "PMl%d" % par, "Wout"], writes=["ps%d" % bt])
                        P.op("dve", lambda h, bt=bt, half=half, sl=sl, ns=ns: h.scalar_tensor_tensor(
                            out=HsT[0:ns, half * 512:(half + 1) * 512], in0=Xt4[sl][0:ns, half * 512:(half + 1) * 512], scalar=DN_ALPHA,
                            in1=PS[bt][0:ns, :], op0=ALU.mult, op1=ALU.add), reads=["ps%d" % bt, "Xt4%d" % sl], writes=[hsn])
                        P.op("dve", lambda h, half=half, sl=sl, ns=ns, STATS=STATS: h.bn_stats(out=STATS[0:ns, half * 6:(half + 1) * 6],
                                                                               in_=HsT[0:ns, half * 512:(half + 1) * 512]),
                             reads=[hsn], writes=["STATS%d" % sl])
                    load_x(sub_i[0] + 1)
                    P.op("dve", lambda h, ns=ns, STATS=STATS, MV=MV: h.bn_aggr(out=MV[0:ns, :], in_=STATS[0:ns, :]), reads=["STATS%d" % sl], writes=["MV%d" % sl])
                    P.op("act", lambda h, ns=ns, MV=MV, RSTD=RSTD: h.activation(out=RSTD[0:ns, :], in_=MV[0:ns, 1:2], func=AF.Sqrt, bias=EPS[0:ns, :]),
                         reads=["MV%d" % sl, "EPS"], writes=["RSTD%d" % sl])
                    P.op("dve", lambda h, ns=ns, RSTD=RSTD: h.reciprocal(out=RSTD[0:ns, :], in_=RSTD[0:ns, :]), reads=["RSTD%d" % sl], writes=["RSTD%d" % sl])
                    P.op("dve", lambda h, sl=sl, ns=ns, MV=MV, RSTD=RSTD: h.tensor_scalar(out=HsT[0:ns, :], in0=HsT[0:ns, :], scalar1=MV[0:ns, 0:1],
                                                                      scalar2=RSTD[0:ns, 0:1], op0=ALU.subtract, op1=ALU.mult),
                         reads=[hsn, "MV%d" % sl, "RSTD%d" % sl], writes=[hsn])
                    P.op("pool", lambda h, sl=sl, ns=ns: h.tensor_tensor(out=HsT[0:ns, :], in0=HsT[0:ns, :], in1=Gam[0:ns, :], op=ALU.mult),
                         reads=[hsn, "Gam"], writes=[hsn])
                    P.op("pool", lambda h, sl=sl, ns=ns: h.tensor_tensor(out=HsT[0:ns, :], in0=HsT[0:ns, :], in1=Bet[0:ns, :], op=ALU.add),
                         reads=[hsn, "Bet"], writes=[hsn])
                    dst = ys if is_s else yp[t0 + sidx * 128: t0 + (sidx + 1) * 128, :]
                    P.dma(dst, HsT[0:ns, :], reads=[hsn], key="outy%d" % hs)
                for sidx_ in range(nsub):
                    one_sub(sidx_)

            def load_wout():
                pass

            all_subs = []
            for (t0_, nt_) in tiles:
                if nt_ == NS:
                    all_subs.append((xs, NS))
                else:
                    for sidx_ in range(4):
                        all_subs.append((xp[t0_ + sidx_ * 128: t0_ + (sidx_ + 1) * 128, :], 128))

            def load_x(k):
                if k >= len(all_subs):
                    return
                src_, ns_ = all_subs[k]
                sl_ = k % 2
                P.dma(Xt4[sl_][0:ns_], src_, writes=["Xt4%d" % sl_], key="xt4%d" % sl_)

            def load_sgpm(ti):
                if ti >= len(tiles):
                    return
                t0_, nt_ = tiles[ti]
                P.dma(SGl[ti % 2][:, :, 0:nt_], SGscr[:, :, t0_:t0_ + nt_], writes=["SGl%d" % (ti % 2)], key="sgl%d" % (ti % 2), queue="act")
                P.dma(PMl[ti % 2][:, :, 0:nt_], PMscr[:, :, t0_:t0_ + nt_], writes=["PMl%d" % (ti % 2)], key="pml%d" % (ti % 2), queue="act")

            load_sgpm(0)
            load_sgpm(1)
            front(0, *tiles[0])
            load_wout()
            load_x(0)
            load_x(1)
            for ti_ in range(1, len(tiles)):
                front(ti_, *tiles[ti_])
                back(ti_ - 1, *tiles[ti_ - 1])
                load_sgpm(ti_ + 1)
            back(len(tiles) - 1, *tiles[-1])
            P.barrier()
            P.emit(blk)
    return nc


_PROG = None


def kernel(**inputs):
    global _PROG
    f = lambda a: np.ascontiguousarray(np.asarray(a, dtype=np.float32))
    xpf = f(inputs["x_prompt"]); xsf = f(inputs["x_sample"])
    sref = f(inputs["state_ssm_re"])[0]; simf = f(inputs["state_ssm_im"])[0]; spf = f(inputs["state_pool"])[0]
    shared = {
        "w_in": f(inputs["w_in"])[0], "lam_re": f(inputs["ssm_lambda_re"])[0], "lam_im": f(inputs["ssm_lambda_im"])[0],
        "log_dt": f(inputs["ssm_log_dt"]).reshape(1, G), "b_re": f(inputs["ssm_b_re"])[0], "b_im": f(inputs["ssm_b_im"])[0],
        "c_re": f(inputs["ssm_c_re"])[0], "c_im": f(inputs["ssm_c_im"])[0], "ssm_d": f(inputs["ssm_d"])[0],
        "glu_w": f(inputs["glu_w"])[0], "glu_b": f(inputs["glu_b"])[0], "pool_w": f(inputs["pool_w"])[0],
        "pool_scale": f(inputs["pool_scale"])[0], "w_out": f(inputs["w_out"])[0],
        "ln_g": f(inputs["ln_g"]).reshape(1, 1024), "ln_b": f(inputs["ln_b"]).reshape(1, 1024),
    }
    in_maps = []
    for c in range(8):
        m = dict(shared)
        m["xp"] = np.ascontiguousarray(xpf[c])
        m["xs"] = np.ascontiguousarray(xsf[c * NS:(c + 1) * NS, 0, :])
        m["sre"] = np.ascontiguousarray(sref[c * NS:(c + 1) * NS])
        m["sim"] = np.ascontiguousarray(simf[c * NS:(c + 1) * NS])
        m["spool"] = np.ascontiguousarray(spf[c * NS:(c + 1) * NS])
        in_maps.append(m)
    if _PROG is None:
        _PROG = build_program()
    res = run_bass_kernel_spmd(_PROG, in_maps, core_ids=list(range(8)))
    R = res.results
    y_prompt = np.stack([R[c]["yp"] for c in range(8)], 0).astype(np.float32)
    y_sample = np.concatenate([R[c]["ys"] for c in range(8)], 0).reshape(128, 1, D_MODEL).astype(np.float32)
    pre = np.stack([R[c]["pre"] for c in range(8)], 0)[None].astype(np.float32)
    pim = np.stack([R[c]["pim"] for c in range(8)], 0)[None].astype(np.float32)
    ppool = np.stack([R[c]["ppool"] for c in range(8)], 0)[None].astype(np.float32)
    sre_o = np.concatenate([R[c]["sre_o"] for c in range(8)], 0)[None].astype(np.float32)
    sim_o = np.concatenate([R[c]["sim_o"] for c in range(8)], 0)[None].astype(np.float32)
    spool_o = np.concatenate([R[c]["spool_o"] for c in range(8)], 0)[None].astype(np.float32)
    return (y_prompt, y_sample, pre, pim, ppool, sre_o, sim_o, spool_o)
```
